# Optimizing a Trainium2 kernel written in Bass

```python
import jax, jax.numpy as jnp
from jax import lax
import numpy as np

D_MODEL = 1024
BATCH = 8
SEQ = 4096
DEPTH = 1

CONV_CH = D_MODEL // 2
CONV_K = 31
CONV_GROUPS = 8
POOL_CH = D_MODEL // 2
POOL_WINDOWS = (2, 4, 8, 16)
N_POOL_GROUPS = len(POOL_WINDOWS)
POOL_GROUP_CH = POOL_CH // N_POOL_GROUPS
POOL_GROUP_OUT = D_MODEL // N_POOL_GROUPS
N_BRANCH = 2
COL_A_VAL = CONV_CH
COL_A_GATE = 2 * CONV_CH
COL_POOL = 2 * CONV_CH + POOL_CH
COL_GATE_A = COL_POOL + D_MODEL
IN_COLS = COL_POOL + N_BRANCH * D_MODEL
D_FF = ((8 * D_MODEL // 3 + 127) // 128) * 128
FFN_K = 3
ALPHA = (2.0 * DEPTH) ** 0.25
BETA = (8.0 * DEPTH) ** -0.25
LN_EPS = 1e-5
N_MOD = 6

kernel_name = "hybrid_conformer_pool_deepnorm_adaln_block"


def layer_norm(x, g=None, b=None):
    xf = x.astype(jnp.float32)
    mu = jnp.mean(xf, axis=-1, keepdims=True)
    var = jnp.mean(jnp.square(xf - mu), axis=-1, keepdims=True)
    y = (xf - mu) * lax.rsqrt(var + LN_EPS)
    if g is not None:
        y = y * g.astype(jnp.float32) + b.astype(jnp.float32)
    return y.astype(x.dtype)


def causal_dwconv(x, w, b):
    k, ch = w.shape
    y = lax.conv_general_dilated(
        x, w[:, None, :].astype(x.dtype), window_strides=(1,),
        padding=[(k - 1, 0)], dimension_numbers=("NWC", "WIO", "NWC"),
        feature_group_count=ch)
    return y + b.astype(x.dtype)


def causal_multiscale_pool(u):
    bsz, s, _ = u.shape
    ug = u.reshape(bsz, s, N_POOL_GROUPS, POOL_GROUP_CH).astype(jnp.float32)
    cs = jnp.cumsum(ug, axis=1)
    pos = jnp.arange(s)
    outs = []
    for gi, w in enumerate(POOL_WINDOWS):
        c_g = cs[:, :, gi]
        lag = jnp.pad(c_g, ((0, 0), (w, 0), (0, 0)))[:, :s]
        cnt = jnp.minimum(pos + 1, w).astype(jnp.float32)[None, :, None]
        outs.append((c_g - lag) / cnt)
    pooled = jnp.stack(outs, axis=2)
    return (pooled - ug).astype(u.dtype)


def setup_inputs(seed: int = 0) -> dict:
    key = jax.random.key(seed)
    ks = jax.random.split(key, 24)
    f32 = jnp.float32
    L = DEPTH

    def nrm(k, shape, scale):
        return jax.random.normal(k, shape, f32) * scale

    def gain(k, shape):
        return 1.0 + 0.05 * jax.random.normal(k, shape, f32)

    return {
        "x": jax.random.normal(ks[0], (BATCH, SEQ, D_MODEL), f32),
        "c": jax.random.normal(ks[1], (BATCH, D_MODEL), f32),
        "w_ada": nrm(ks[2], (L, D_MODEL, N_MOD * D_MODEL), 0.5 * D_MODEL ** -0.5),
        "b_ada": nrm(ks[3], (L, N_MOD * D_MODEL), 0.02),
        "w_in": nrm(ks[4], (L, D_MODEL, IN_COLS), D_MODEL ** -0.5),
        "w_dw_a": nrm(ks[5], (L, CONV_K, CONV_CH), CONV_K ** -0.5),
        "b_dw_a": nrm(ks[6], (L, CONV_CH), 0.02),
        "ln_a_g": gain(ks[7], (L, CONV_CH)),
        "ln_a_b": nrm(ks[8], (L, CONV_CH), 0.02),
        "w_pw_a": nrm(ks[9], (L, CONV_CH, D_MODEL), BETA * CONV_CH ** -0.5),
        "w_pool": nrm(ks[10], (L, N_POOL_GROUPS, POOL_GROUP_CH, POOL_GROUP_OUT), BETA * POOL_GROUP_CH ** -0.5),
        "pool_scale": gain(ks[11], (L, D_MODEL)),
        "w_out": nrm(ks[12], (L, D_MODEL, D_MODEL), BETA * D_MODEL ** -0.5),
        "ln1_g": gain(ks[13], (L, D_MODEL)),
        "ln1_b": nrm(ks[14], (L, D_MODEL), 0.02),
        "w_up": nrm(ks[15], (L, D_MODEL, 2 * D_FF), BETA * D_MODEL ** -0.5),
        "w_dw_f": nrm(ks[16], (L, FFN_K, 2 * D_FF), FFN_K ** -0.5),
        "b_dw_f": nrm(ks[17], (L, 2 * D_FF), 0.02),
        "w_down": nrm(ks[18], (L, D_FF, D_MODEL), BETA * D_FF ** -0.5),
        "ln2_g": gain(ks[19], (L, D_MODEL)),
        "ln2_b": nrm(ks[20], (L, D_MODEL), 0.02),
    }


def reference(x, c, w_ada, b_ada, w_in, w_dw_a, b_dw_a, ln_a_g, ln_a_b, w_pw_a,
              w_pool, pool_scale, w_out, ln1_g, ln1_b, w_up, w_dw_f, b_dw_f,
              w_down, ln2_g, ln2_b):
    bsz, s, _ = x.shape
    for l in range(DEPTH):
        mod = jax.nn.silu(c) @ w_ada[l] + b_ada[l]
        sh1, sc1, g1, sh2, sc2, g2 = jnp.split(mod, N_MOD, axis=-1)

        h = layer_norm(x) * (1.0 + sc1[:, None]) + sh1[:, None]
        proj = h @ w_in[l]
        a_val, a_gate, u_pool, gate_a, gate_b = jnp.split(
            proj, [COL_A_VAL, COL_A_GATE, COL_POOL, COL_GATE_A], axis=-1)

        ya = a_val * jax.nn.sigmoid(a_gate)
        ya = causal_dwconv(ya, w_dw_a[l], b_dw_a[l])
        ya = jax.nn.silu(layer_norm(ya, ln_a_g[l], ln_a_b[l]))
        ya = ya @ w_pw_a[l]

        pooled = causal_multiscale_pool(u_pool)
        yb = jnp.einsum("bsgc,gcd->bsgd", pooled, w_pool[l]).reshape(bsz, s, D_MODEL)
        yb = yb * pool_scale[l]

        merged = jax.nn.sigmoid(gate_a) * ya + jax.nn.sigmoid(gate_b) * yb
        mix = merged @ w_out[l]
        x = layer_norm(ALPHA * x + g1[:, None] * mix, ln1_g[l], ln1_b[l])

        h = layer_norm(x) * (1.0 + sc2[:, None]) + sh2[:, None]
        up = causal_dwconv(h @ w_up[l], w_dw_f[l], b_dw_f[l])
        v, gt = jnp.split(up, 2, axis=-1)
        f = (jax.nn.gelu(gt) * v) @ w_down[l]
        x = layer_norm(ALPHA * x + g2[:, None] * f, ln2_g[l], ln2_b[l])
    return x
```

```python
import numpy as np
from contextlib import ExitStack
import concourse.bass as bass
import concourse.mybir as mybir
from concourse.bass_utils import run_bass_kernel_spmd

F32 = mybir.dt.float32
BF16 = mybir.dt.bfloat16
AF = mybir.ActivationFunctionType
ALU = mybir.AluOpType

P = 128
D = 1024
S = 4096
T = 512
NT = S // T
NSUB = T // P
KC = D // P
CONV_K = 31
DFF = 2816
NFF = DFF // P
ALPHA = 2.0 ** 0.25
EPS = 1e-5
NRING = 7
NCHUNK_TILE = 12 + 16 + 8 + 44 + 22

C_BDWA = 0
C_LNAG = 4
C_LNAB = 8
C_PSC = 12
C_WDWA = 20
C_WDWF = 144
C_BDWF = 276
C_C = 320
C_RCNT = 328
NCOLS = 392


class Buf:
    __slots__ = ("name", "w", "rs")

    def __init__(self, name):
        self.name = name
        self.w = None
        self.rs = []


class Op:
    __slots__ = ("eng", "fn", "waits", "signal", "sigval", "chan", "chanval")

    def __init__(self, eng, fn):
        self.eng = eng
        self.fn = fn
        self.waits = []
        self.signal = False
        self.sigval = 0
        self.chan = None
        self.chanval = 0


ENGS = ("pe", "act", "dve", "pool", "sp")


class Sched:
    def __init__(self):
        self.ops = {e: [] for e in ENGS}
        self.chan_cnt = {}

    def add(self, eng, fn, reads=(), writes=(), chan=None):
        op = Op(eng, fn)
        if chan is not None:
            op.chan = chan
            self.chan_cnt[chan] = self.chan_cnt.get(chan, 0) + 16
            op.chanval = self.chan_cnt[chan]
        deps = []
        for b in reads:
            if b.w is not None:
                deps.append((b.w, True))
        for b in writes:
            if b.w is not None:
                deps.append((b.w, False))
            last = {}
            for r in b.rs:
                last[r.chan if r.chan is not None else r.eng] = r
            for r in last.values():
                deps.append((r, False))
        seen = set()
        for (d, raw) in deps:
            if id(d) in seen:
                continue
            if d.chan is None and op.chan is None and d.eng == eng:
                if eng == "pe":
                    continue
            seen.add(id(d))
            if d.chan is None:
                d.signal = True
            op.waits.append(d)
        for b in reads:
            b.rs.append(op)
        for b in writes:
            b.w = op
            b.rs = []
        self.ops[eng].append(op)
        return op

    def finalize(self):
        for e in ENGS:
            n = 0
            for op in self.ops[e]:
                if op.chan is None and op.signal:
                    n += 1
                    op.sigval = n


def build_nc(nt=NT):
    S = nt * T
    nc = bass.Bass("TRN2", target_bir_lowering=False)

    def din(name, shape, dt=F32):
        return nc.dram_tensor(name, list(shape), dt, kind="ExternalInput").ap()

    x_d = din("x", [S, D])
    cols_d = din("cols", [P, NCOLS])
    ident_d = din("ident", [P, P])
    wada_d = din("wada", [P, 6, KC * D])
    bada_d = din("bada", [6, D])
    wst_d = din("wstream", [NCHUNK_TILE * P, D])
    wpw_d = din("wpw", [P, 4 * D])
    wpool_d = din("wpool", [P, D])
    lnv_d = din("lnv", [4, D])
    out_d = nc.dram_tensor("out", [S, D], F32, kind="ExternalOutput").ap()
    wsc_d = nc.dram_tensor("wsc", [NCHUNK_TILE * P, D], BF16, kind="Internal").ap()

    sc = Sched()
    es = ExitStack()

    def sb(name, shape, dt=F32):
        return es.enter_context(nc.sbuf_tensor("sb_" + name, list(shape), dt))

    cols = sb("cols", [P, NCOLS])
    identf = sb("identf", [P, P])
    identb = sb("identb", [P, P], BF16)
    ones_f = sb("ones_f", [P, P])
    ones_row = sb("ones_row", [1, P])
    eps_t = sb("eps_t", [P, 1])
    mhalf_t = sb("mhalf_t", [P, 1])
    diag = sb("diag", [P, CONV_K * 4, P], BF16)
    g1b = sb("g1b", [P, D])
    g2b = sb("g2b", [P, D])
    lnb = sb("lnb", [P, 4, D])
    modcols = sb("modcols", [P, 4, KC])
    wpw = sb("wpw", [P, 4, D], BF16)
    wpool = sb("wpool", [P, D], BF16)
    s_bf = sb("s_bf", [P, KC], BF16)
    xt_all = sb("xt", [P, 2, NSUB, D])
    xn = sb("xn", [P, NSUB, D], BF16)
    hT = sb("hT", [P, KC, T + 2], BF16)
    glu = sb("glu", [P, 4, T + 30], BF16)
    ybuf = sb("ybuf", [P, 4, T])
    mrow = ybuf[0:1, 0:2, :].rearrange("p a b -> p (a b)")
    brow = ybuf[0:1, 2:4, :].rearrange("p a b -> p (a b)")
    scr = sb("scr", [P, 8, T + 16])
    zT = sb("zT", [P, 4, T], BF16)
    ubuf = sb("ubuf", [P, 4, T + 15])
    pooledT = sb("pooledT", [P, 4, T], BF16)
    merged = sb("merged", [P, KC, T], BF16)
    gT = sb("gT", [P, NFF * T], BF16)
    stash = sb("stash", [P, 2 * NFF, 2])
    carry = sb("carry", [P, 2 * NFF, 2])
    ctmp = sb("ctmp", [P, 2 * NFF])
    ring = sb("ring", [P, NRING, D], BF16)
    st = sb("st", [P, NSUB, 2, 6])
    mv = sb("mv", [P, NSUB, 2])
    sums = sb("sums", [P, NSUB, 2])
    sd4 = sb("sd4", [P, NSUB])
    rstd4 = sb("rstd4", [P, NSUB])
    nb4 = sb("nb4", [P, NSUB])
    ps = es.enter_context(nc.psum_tensor("ps", [P, 8, 512], F32))

    sems = {e: es.enter_context(nc.semaphore("sem_" + e)) for e in ENGS}
    chan_sems = {}

    def chan_sem(name):
        if name not in chan_sems:
            chan_sems[name] = es.enter_context(nc.semaphore("ch_" + name))
        return chan_sems[name]

    B = {}

    def buf(name):
        if name not in B:
            B[name] = Buf(name)
        return B[name]

    bank = [buf("bank%d" % i) for i in range(8)]
    bank_rr = [0]

    def next_bank():
        b = bank_rr[0]
        bank_rr[0] = (b + 1) % 8
        return b

    gT3 = gT[:, :].rearrange("p (n t) -> p n t", n=NFF)
    stg = gT[:, 0:KC * D].rearrange("p (k n) -> p k n", k=KC)
    stgb = [buf("gT%d" % q) for q in range(KC * D // T)]
    glub = [buf("glu%d" % q) for q in range(4)]
    ubufb = [buf("ubuf%d" % q) for q in range(4)]

    sc.add("sp", lambda e: e.dma_start(out=cols[:, :], in_=cols_d), writes=[buf("cols")], chan="su0")
    sc.add("sp", lambda e: e.dma_start(out=identf[:, :], in_=ident_d), writes=[buf("identf")], chan="su1")
    for i in range(4):
        sc.add("sp", lambda e, i=i: e.dma_start(out=lnb[:, i, :], in_=lnv_d[i:i + 1, :].broadcast_to([P, D])),
               writes=[buf("lnb")], chan="su2")
    sc.add("pool", lambda e: e.memset(ones_f[:, :], 1.0 / 512.0), writes=[buf("ones_f")])
    sc.add("pool", lambda e: e.memset(ones_row[:, :], 1.0), writes=[buf("ones_row")])
    sc.add("pool", lambda e: e.memset(eps_t[:, :], EPS), writes=[buf("eps_t")])
    sc.add("pool", lambda e: e.memset(mhalf_t[:, :], -0.5), writes=[buf("mhalf_t")])
    sc.add("pool", lambda e: e.dma_start(out=wpw[:, :, :], in_=wpw_d.rearrange("p (c d) -> p c d", c=4)),
           writes=[buf("wpw")], chan="sg0")
    sc.add("pool", lambda e: e.dma_start(out=wpool[:, :], in_=wpool_d), writes=[buf("wpool")], chan="sg1")
    PIECE_CH = 3
    NPIECE = NCHUNK_TILE // PIECE_CH
    wsc_piece = [buf("wscp%d" % i) for i in range(NPIECE)]
    cast_state = {"n": 0}

    def cast_piece(_i, count=6):
        for _ in range(count):
            i = cast_state["n"]
            if i >= NPIECE:
                return
            cast_state["n"] = i + 1
            r0 = i * PIECE_CH * P
            r1 = (i + 1) * PIECE_CH * P
            sc.add("pool", lambda e, r0=r0, r1=r1: e.dma_start(out=wsc_d[r0:r1, :], in_=wst_d[r0:r1, :]),
                   writes=[wsc_piece[i]], chan="wc%d" % i)

    sc.add("act", lambda e: e.activation(out=identb[:, :], in_=identf[:, :], func=AF.Copy),
           reads=[buf("identf")], writes=[buf("identb")])
    sc.add("act", lambda e: e.activation(out=s_bf[:, :], in_=cols[:, C_C:C_C + KC], func=AF.Silu),
           reads=[buf("cols")], writes=[buf("s_bf")])

    def mod_issue(p):
        sc.add("pool", lambda e: e.dma_start(out=stg, in_=wada_d[:, p, :].rearrange("p (k n) -> p k n", k=KC)),
               writes=stgb, chan="wada")

    def mod_consume(p):
        sc.add("sp", lambda e: e.dma_start(out=brow, in_=bada_d[p:p + 1, :]), writes=[buf("ybuf2"), buf("ybuf3")], chan="su3")
        for h in range(2):
            b = next_bank()
            for kc in range(KC):
                sc.add("pe", lambda e, b=b, kc=kc, h=h: e.matmul(
                    ps[0:1, b, :], lhsT=s_bf[:, kc:kc + 1], rhs=stg[:, kc, h * 512:(h + 1) * 512],
                    start=(kc == 0), stop=(kc == KC - 1)),
                    reads=[buf("s_bf")] + stgb, writes=[bank[b]])
            sc.add("dve", lambda e, b=b, h=h: e.tensor_tensor(
                out=mrow[:, h * 512:(h + 1) * 512], in0=ps[0:1, b, :], in1=brow[:, h * 512:(h + 1) * 512],
                op=ALU.add), reads=[bank[b], buf("ybuf2"), buf("ybuf3")], writes=[buf("ybuf0"), buf("ybuf1")])
        if p in (2, 5):
            dst = g1b if p == 2 else g2b
            for h in range(2):
                b = next_bank()
                sc.add("pe", lambda e, b=b, h=h: e.matmul(
                    ps[:, b, :], lhsT=ones_row[0:1, :], rhs=mrow[:, h * 512:(h + 1) * 512], start=True, stop=True),
                    reads=[buf("ones_row"), buf("ybuf0"), buf("ybuf1")], writes=[bank[b]])
                sc.add("act", lambda e, b=b, h=h, dst=dst: e.activation(
                    out=dst[:, h * 512:(h + 1) * 512], in_=ps[:, b, :], func=AF.Copy),
                    reads=[bank[b]], writes=[buf("gb")])
        else:
            idx = {0: 1, 1: 0, 3: 3, 4: 2}[p]
            b = next_bank()
            for kc in range(KC):
                sc.add("pe", lambda e, b=b, kc=kc: e.matmul(
                    ps[:, b, kc:kc + 1], lhsT=mrow[:, kc * P:(kc + 1) * P], rhs=ones_row[0:1, 0:1],
                    start=True, stop=True),
                    reads=[buf("ones_row"), buf("ybuf0"), buf("ybuf1")], writes=[bank[b]])
            addc = 1.0 if p in (1, 4) else 0.0
            sc.add("dve", lambda e, b=b, idx=idx, addc=addc: e.tensor_scalar(
                out=modcols[:, idx, :], in0=ps[:, b, 0:KC], scalar1=addc, scalar2=None, op0=ALU.add),
                reads=[bank[b]], writes=[buf("modcols")])

    mod_issue(0)
    cast_piece(0, 4)
    for j in range(CONV_K):
        for c in range(4):
            i = j * 4 + c
            sc.add("act", lambda e, i=i: e.activation(
                out=diag[:, i, :], in_=identf[:, :], func=AF.Copy, scale=cols[:, C_WDWA + i:C_WDWA + i + 1]),
                reads=[buf("identf"), buf("cols")], writes=[buf("diag")])

    mod_consume(0)
    mod_issue(1)
    cast_piece(0, 8)
    mod_consume(1)
    mod_issue(2)
    cast_piece(0, 22)

    ring_state = {"n": 0}
    ring_buf = [buf("ring%d" % i) for i in range(NRING)]

    def ring_load():
        n = ring_state["n"]
        ring_state["n"] = n + 1
        slot = n % NRING
        k = n % NCHUNK_TILE
        piece = wsc_piece[k // PIECE_CH]
        sc.add("sp", lambda e: e.dma_start(out=ring[:, slot, :], in_=wsc_d[k * P:(k + 1) * P, :]),
               reads=[piece], writes=[ring_buf[slot]], chan="ring%d" % slot)
        return slot

    pending = []
    PREFETCH = 5
    total_chunks = NCHUNK_TILE * nt

    def next_w():
        while len(pending) < PREFETCH + 1 and ring_state["n"] < total_chunks:
            pending.append(ring_load())
        return pending.pop(0)

    def ln_stats(xt, xtbs):
        for s in range(NSUB):
            sc.add("act", lambda e, s=s: e.activation(out=xn[:, s, :], in_=xt[:, s, :], func=AF.Copy,
                                                       accum_out=sums[:, s, 0:1]),
                   reads=[xtbs[s]], writes=[buf("xn"), buf("sums")])
            sc.add("act", lambda e, s=s: e.activation(out=xn[:, s, :], in_=xt[:, s, :], func=AF.Square,
                                                       accum_out=sums[:, s, 1:2]),
                   reads=[xtbs[s]], writes=[buf("xn"), buf("sums")])
        sc.add("act", lambda e: e.activation(out=mv[:, :, 0], in_=sums[:, :, 0], func=AF.Copy, scale=1.0 / D),
               reads=[buf("sums")], writes=[buf("mv")])
        sc.add("dve", lambda e: e.tensor_tensor(out=sd4[:, :], in0=mv[:, :, 0], in1=mv[:, :, 0], op=ALU.mult),
               reads=[buf("mv")], writes=[buf("sd4")])
        sc.add("dve", lambda e: e.scalar_tensor_tensor(
            out=mv[:, :, 1], in0=sums[:, :, 1], scalar=1.0 / D, in1=sd4[:, :], op0=ALU.mult, op1=ALU.subtract),
            reads=[buf("sums"), buf("sd4")], writes=[buf("mv")])
        sc.add("act", lambda e: e.activation(out=sd4[:, :], in_=mv[:, :, 1], func=AF.Ln,
                                              bias=eps_t[:, 0:1], scale=1.0),
               reads=[buf("mv"), buf("eps_t")], writes=[buf("sd4")])
        sc.add("act", lambda e: e.activation(out=rstd4[:, :], in_=sd4[:, :], func=AF.Exp, scale=-0.5),
               reads=[buf("sd4")], writes=[buf("rstd4")])
        sc.add("dve", lambda e: e.scalar_tensor_tensor(
            out=nb4[:, :], in0=mv[:, :, 0], scalar=-1.0, in1=rstd4[:, :], op0=ALU.mult, op1=ALU.mult),
            reads=[buf("mv"), buf("rstd4")], writes=[buf("nb4")])

    def to_feature_major(mod_scale_idx, mod_shift_idx):
        for kc in range(KC):
            b = next_bank()
            pb = ps[:, b, :].bitcast(BF16)
            for s in range(NSUB):
                sc.add("pe", lambda e, pb=pb, s=s, kc=kc: e.transpose(
                    pb[:, s * P:(s + 1) * P], xn[:, s, kc * P:(kc + 1) * P], identb[:, :]),
                    reads=[buf("xn"), buf("identb")], writes=[bank[b]])
            eng = "act" if kc % 2 == 0 else "dve"
            if eng == "act":
                sc.add("act", lambda e, pb=pb, kc=kc: e.activation(
                    out=hT[:, kc, 2:2 + T], in_=pb[:, 0:T], func=AF.Identity,
                    scale=modcols[:, mod_scale_idx, kc:kc + 1], bias=modcols[:, mod_shift_idx, kc:kc + 1]),
                    reads=[bank[b], buf("modcols")], writes=[buf("hT%d" % kc)])
            else:
                sc.add("dve", lambda e, pb=pb, kc=kc: e.tensor_scalar(
                    out=hT[:, kc, 2:2 + T], in0=pb[:, 0:T],
                    scalar1=modcols[:, mod_scale_idx, kc:kc + 1], scalar2=modcols[:, mod_shift_idx, kc:kc + 1],
                    op0=ALU.mult, op1=ALU.add),
                    reads=[bank[b], buf("modcols")], writes=[buf("hT%d" % kc)])

    hT_bufs = [buf("hT%d" % kc) for kc in range(KC)]

    def proj_chunk(b, slot, rhs_lo=2):
        for kc in range(KC):
            sc.add("pe", lambda e, kc=kc: e.matmul(
                ps[:, b, :], lhsT=ring[:, slot, kc * P:(kc + 1) * P], rhs=hT[:, kc, rhs_lo:rhs_lo + T],
                start=(kc == 0), stop=(kc == KC - 1)),
                reads=[ring_buf[slot], hT_bufs[kc]], writes=[bank[b]])

    def front_ln(it):
        t0 = it * T
        xt = xt_all[:, it % 2]
        xtbs = [buf("xt%d_%d" % (it % 2, q)) for q in range(NSUB)]
        sc.add("sp", lambda e, t0=t0: e.dma_start(
            out=xt[:, :, :], in_=x_d[t0:t0 + T, :].rearrange("(s p) d -> p s d", p=P)),
            writes=xtbs, chan="x%d" % (it % 2))
        ln_stats(xt, xtbs)
        for s in range(NSUB):
            sc.add("act", lambda e, s=s: e.activation(
                out=xn[:, s, :], in_=xt[:, s, :], func=AF.Identity, scale=rstd4[:, s:s + 1], bias=nb4[:, s:s + 1]),
                reads=[xtbs[s], buf("rstd4"), buf("nb4")], writes=[buf("xn")])

    def do_tile(it):
        t0 = it * T
        xt = xt_all[:, it % 2]
        xtbs = [buf("xt%d_%d" % (it % 2, q)) for q in range(NSUB)]
        if it == 0:
            front_ln(0)
            to_feature_major(0, 1)

        if it == 0:
            sc.add("pool", lambda e: e.memset(glu[:, :, 0:30], 0.0), writes=glub)
            sc.add("pool", lambda e: e.memset(ubuf[:, :, 0:15], 0.0), writes=ubufb)
        else:
            sc.add("pool", lambda e: e.tensor_copy(out=glu[:, :, 0:30], in_=glu[:, :, T:T + 30]),
                   reads=glub, writes=glub)
            sc.add("pool", lambda e: e.tensor_copy(out=ubuf[:, :, 0:15], in_=ubuf[:, :, T:T + 15]),
                   reads=ubufb, writes=ubufb)

        for c in range(4):
            bv = next_bank()
            proj_chunk(bv, next_w())
            bg = next_bank()
            proj_chunk(bg, next_w())
            sc.add("act", lambda e, bg=bg: e.activation(out=scr[:, 0, 0:T], in_=ps[:, bg, :], func=AF.Sigmoid),
                   reads=[bank[bg]], writes=[buf("scr0")])
            sc.add("dve", lambda e, bv=bv, c=c: e.tensor_tensor(
                out=glu[:, c, 30:30 + T], in0=ps[:, bv, :], in1=scr[:, 0, 0:T], op=ALU.mult),
                reads=[bank[bv], buf("scr0")], writes=[glub[c]])
        for g in range(4):
            slot = next_w()
            b = next_bank()
            proj_chunk(b, slot)
            sc.add("act", lambda e, b=b, g=g: e.activation(out=ubuf[:, g, 15:15 + T], in_=ps[:, b, :], func=AF.Copy),
                   reads=[bank[b]], writes=[ubufb[g]])
        if it == 0:
            mod_consume(2)
            mod_issue(3)
        for c in range(4):
            b = next_bank()
            for j in range(CONV_K):
                sc.add("pe", lambda e, b=b, c=c, j=j: e.matmul(
                    ps[:, b, :], lhsT=diag[:, j * 4 + c, :], rhs=glu[:, c, j:j + T],
                    start=(j == 0), stop=(j == CONV_K - 1)),
                    reads=[buf("diag"), glub[c]], writes=[bank[b]])
            sc.add("act", lambda e, b=b, c=c: e.activation(
                out=ybuf[:, c, :], in_=ps[:, b, :], func=AF.Identity, bias=cols[:, C_BDWA + c:C_BDWA + c + 1], scale=1.0),
                reads=[bank[b], buf("cols")], writes=[buf("ybuf%d" % c)])
            sc.add("act", lambda e, b=b, c=c: e.activation(
                out=scr[:, c % 2, 0:T], in_=ps[:, b, :], func=AF.Square, bias=cols[:, C_BDWA + c:C_BDWA + c + 1], scale=1.0),
                reads=[bank[b], buf("cols")], writes=[buf("scr%d" % (c % 2))])
            if c == 0:
                bm = next_bank()
                bq = next_bank()
            sc.add("pe", lambda e, c=c, bm=bm: e.matmul(
                ps[:, bm, :], lhsT=ones_f[:, :], rhs=ybuf[:, c, :], start=(c == 0), stop=(c == 3)),
                reads=[buf("ones_f"), buf("ybuf%d" % c)], writes=[bank[bm]])
            sc.add("pe", lambda e, c=c, bq=bq: e.matmul(
                ps[:, bq, :], lhsT=ones_f[:, :], rhs=scr[:, c % 2, 0:T], start=(c == 0), stop=(c == 3)),
                reads=[buf("ones_f"), buf("scr%d" % (c % 2))], writes=[bank[bq]])
        sc.add("act", lambda e, bm=bm: e.activation(out=scr[:, 4, 0:T], in_=ps[:, bm, :], func=AF.Copy),
               reads=[bank[bm]], writes=[buf("scr4")])
        sc.add("act", lambda e, bm=bm: e.activation(out=scr[:, 2, 0:T], in_=ps[:, bm, :], func=AF.Square),
               reads=[bank[bm]], writes=[buf("scr2")])
        sc.add("dve", lambda e, bq=bq: e.tensor_tensor(out=scr[:, 3, 0:T], in0=ps[:, bq, :], in1=scr[:, 2, 0:T],
                                                        op=ALU.subtract),
               reads=[bank[bq], buf("scr2")], writes=[buf("scr3")])
        sc.add("act", lambda e: e.activation(out=scr[:, 2, 0:T], in_=scr[:, 3, 0:T], func=AF.Ln,
                                              bias=eps_t[:, 0:1], scale=1.0),
               reads=[buf("scr3"), buf("eps_t")], writes=[buf("scr2")])
        sc.add("act", lambda e: e.activation(out=scr[:, 5, 0:T], in_=scr[:, 2, 0:T], func=AF.Exp, scale=-0.5),
               reads=[buf("scr2")], writes=[buf("scr5")])
        for c in range(4):
            sc.add("dve", lambda e, c=c: e.tensor_tensor(out=scr[:, 2 + c % 2, 0:T], in0=ybuf[:, c, :], in1=scr[:, 4, 0:T],
                                                          op=ALU.subtract),
                   reads=[buf("ybuf%d" % c), buf("scr4")], writes=[buf("scr%d" % (2 + c % 2))])
            sc.add("dve", lambda e, c=c: e.tensor_tensor(out=scr[:, 2 + c % 2, 0:T], in0=scr[:, 2 + c % 2, 0:T], in1=scr[:, 5, 0:T],
                                                          op=ALU.mult),
                   reads=[buf("scr%d" % (2 + c % 2)), buf("scr5")], writes=[buf("scr%d" % (2 + c % 2))])
            sc.add("act", lambda e, c=c: e.activation(
                out=zT[:, c, :], in_=scr[:, 2 + c % 2, 0:T], func=AF.Silu,
                scale=cols[:, C_LNAG + c:C_LNAG + c + 1], bias=cols[:, C_LNAB + c:C_LNAB + c + 1]),
                reads=[buf("scr%d" % (2 + c % 2)), buf("cols")], writes=[buf("zT%d" % c)])
        for g in range(4):
            w = 2 << g
            src = None
            for k in range(g + 1):
                sh = 1 << k
                lo = (2 << k) - 1
                dst = scr[:, 6 + k % 2, 0:T + 15]
                a = ubuf[:, g, :] if k == 0 else scr[:, 6 + (k - 1) % 2, 0:T + 15]
                rd = [ubufb[g]] if k == 0 else [buf("scr%d" % (6 + (k - 1) % 2))]
                sc.add("pool", lambda e, dst=dst, a=a, lo=lo, sh=sh: e.tensor_tensor(
                    out=dst[:, lo:T + 15], in0=a[:, lo:T + 15], in1=a[:, lo - sh:T + 15 - sh], op=ALU.add),
                    reads=rd, writes=[buf("scr%d" % (6 + k % 2))])
                src = (dst, buf("scr%d" % (6 + k % 2)))
            sc.add("dve", lambda e, g=g, w=w, src=src: e.scalar_tensor_tensor(
                out=pooledT[:, g, :], in0=src[0][:, 15:15 + T], scalar=1.0 / w, in1=ubuf[:, g, 15:15 + T],
                op0=ALU.mult, op1=ALU.subtract),
                reads=[src[1], ubufb[g]], writes=[buf("pooledT%d" % g)])
            if it == 0:
                n = w - 1
                sc.add("pool", lambda e, g=g, n=n, src=src: e.tensor_tensor(
                    out=src[0][:, 15:15 + n], in0=src[0][:, 15:15 + n],
                    in1=cols[:, C_RCNT + g * 16:C_RCNT + g * 16 + n], op=ALU.mult),
                    reads=[src[1], buf("cols")], writes=[src[1]])
                sc.add("pool", lambda e, g=g, n=n, src=src: e.tensor_tensor(
                    out=pooledT[:, g, 0:n], in0=src[0][:, 15:15 + n], in1=ubuf[:, g, 15:15 + n], op=ALU.subtract),
                    reads=[src[1], ubufb[g]], writes=[buf("pooledT%d" % g)])
        for jj in range(0, KC, 2):
            gate_banks = {}
            for j in (jj, jj + 1):
                sa_slot = next_w()
                ba = next_bank()
                proj_chunk(ba, sa_slot)
                sb_slot = next_w()
                bb = next_bank()
                proj_chunk(bb, sb_slot)
                gate_banks[j] = (ba, bb)
            for j in (jj, jj + 1):
                ba, bb = gate_banks[j]
                q = 4 * (j % 2)
                bc = next_bank()
                for c in range(4):
                    sc.add("pe", lambda e, c=c, j=j, bc=bc: e.matmul(
                        ps[:, bc, :], lhsT=wpw[:, c, j * P:(j + 1) * P], rhs=zT[:, c, :], start=(c == 0), stop=(c == 3)),
                        reads=[buf("wpw"), buf("zT%d" % c)], writes=[bank[bc]])
                bd = next_bank()
                g = j // 2
                sc.add("pe", lambda e, j=j, g=g, bd=bd: e.matmul(
                    ps[:, bd, :], lhsT=wpool[:, g * 256 + (j % 2) * P:g * 256 + (j % 2) * P + P], rhs=pooledT[:, g, :],
                    start=True, stop=True),
                    reads=[buf("wpool"), buf("pooledT%d" % g)], writes=[bank[bd]])
                sc.add("act", lambda e, ba=ba, q=q: e.activation(out=scr[:, q + 0, 0:T], in_=ps[:, ba, :], func=AF.Sigmoid),
                       reads=[bank[ba]], writes=[buf("scr%d" % (q + 0))])
                sc.add("act", lambda e, bb=bb, q=q: e.activation(out=scr[:, q + 1, 0:T], in_=ps[:, bb, :], func=AF.Sigmoid),
                       reads=[bank[bb]], writes=[buf("scr%d" % (q + 1))])
                sc.add("dve", lambda e, bc=bc, q=q: e.tensor_tensor(out=scr[:, q + 2, 0:T], in0=ps[:, bc, :],
                                                                      in1=scr[:, q + 0, 0:T], op=ALU.mult),
                       reads=[bank[bc], buf("scr%d" % (q + 0))], writes=[buf("scr%d" % (q + 2))])
                sc.add("dve", lambda e, bd=bd, j=j, q=q: e.scalar_tensor_tensor(
                    out=scr[:, q + 3, 0:T], in0=ps[:, bd, :], scalar=cols[:, C_PSC + j:C_PSC + j + 1],
                    in1=scr[:, q + 1, 0:T], op0=ALU.mult, op1=ALU.mult),
                    reads=[bank[bd], buf("scr%d" % (q + 1)), buf("cols")], writes=[buf("scr%d" % (q + 3))])
                sc.add("pool", lambda e, j=j, q=q: e.tensor_tensor(out=merged[:, j, :], in0=scr[:, q + 2, 0:T],
                                                                    in1=scr[:, q + 3, 0:T], op=ALU.add),
                       reads=[buf("scr%d" % (q + 2)), buf("scr%d" % (q + 3))], writes=[buf("merged%d" % j)])
        if it == 0:
            mod_consume(3)
            mod_issue(4)
        acc = list(range(8))
        for kc in range(KC):
            slot = next_w()
            for s in range(NSUB):
                for h in range(2):
                    b = acc[s * 2 + h]
                    sc.add("pe", lambda e, kc=kc, s=s, h=h, b=b, slot=slot: e.matmul(
                        ps[:, b, :], lhsT=merged[:, kc, s * P:(s + 1) * P], rhs=ring[:, slot, h * 512:(h + 1) * 512],
                        start=(kc == 0), stop=(kc == KC - 1)),
                        reads=[ring_buf[slot], buf("merged%d" % kc)], writes=[bank[b]])
        for s in range(NSUB):
            for h in range(2):
                b = acc[s * 2 + h]
                sc.add("dve", lambda e, s=s, h=h, b=b: e.tensor_tensor(
                    out=scr[:, 2 + h, 0:T], in0=ps[:, b, :], in1=g1b[:, h * 512:(h + 1) * 512], op=ALU.mult),
                    reads=[bank[b], buf("gb")], writes=[buf("scr%d" % (2 + h))])
                sc.add("dve", lambda e, s=s, h=h: e.scalar_tensor_tensor(
                    out=xt[:, s, h * 512:(h + 1) * 512], in0=xt[:, s, h * 512:(h + 1) * 512], scalar=ALPHA,
                    in1=scr[:, 2 + h, 0:T], op0=ALU.mult, op1=ALU.add),
                    reads=[xtbs[s], buf("scr%d" % (2 + h))], writes=[xtbs[s]])
        if it == 0:
            mod_consume(4)
            mod_issue(5)
        ln_stats(xt, xtbs)
        for s in range(NSUB):
            sc.add("dve", lambda e, s=s: e.scalar_tensor_tensor(
                out=xt[:, s, :], in0=xt[:, s, :], scalar=mv[:, s, 0:1], in1=lnb[:, 0, :],
                op0=ALU.subtract, op1=ALU.mult),
                reads=[xtbs[s], buf("mv"), buf("lnb")], writes=[xtbs[s]])
            sc.add("dve", lambda e, s=s: e.scalar_tensor_tensor(
                out=xt[:, s, :], in0=xt[:, s, :], scalar=rstd4[:, s:s + 1], in1=lnb[:, 1, :],
                op0=ALU.mult, op1=ALU.add),
                reads=[xtbs[s], buf("rstd4"), buf("lnb")], writes=[xtbs[s]])
        if it == 0:
            sc.add("pool", lambda e: e.memset(hT[:, :, 0:2], 0.0), writes=hT_bufs)
        ln_stats(xt, xtbs)
        for s in range(NSUB):
            sc.add("act", lambda e, s=s: e.activation(
                out=xn[:, s, :], in_=xt[:, s, :], func=AF.Identity, scale=rstd4[:, s:s + 1], bias=nb4[:, s:s + 1]),
                reads=[xtbs[s], buf("rstd4"), buf("nb4")], writes=[buf("xn")])
        to_feature_major(2, 3)

        if it == 0:
            mod_consume(5)
        def ffn_finish(n):
            yv = n % 2
            yg = 2 + (n % 2)
            sc.add("act", lambda e, n=n, yg=yg: e.activation(out=scr[:, 4 + n % 2, 0:T], in_=scr[:, yg, 0:T],
                                                               func=AF.Gelu_apprx_tanh),
                   reads=[buf("scr%d" % yg)], writes=[buf("scr%d" % (4 + n % 2))])
            sc.add("pool", lambda e, n=n, yv=yv: e.tensor_tensor(
                out=gT3[:, n, :], in0=scr[:, 4 + n % 2, 0:T], in1=scr[:, yv, 0:T], op=ALU.mult),
                reads=[buf("scr%d" % (4 + n % 2)), buf("scr%d" % yv)], writes=[buf("gT%d" % n)])

        for n in range(NFF):
            if n == 1 and it + 1 < nt:
                front_ln(it + 1)
            sv = next_w()
            sg = next_w()
            bv = next_bank()
            proj_chunk(bv, sv)
            bg = next_bank()
            proj_chunk(bg, sg)
            for (which, b, ch) in ((0, bv, n), (1, bg, NFF + n)):
                yi = which * 2 + (n % 2)
                yb_ = buf("scr%d" % yi)
                w0 = cols[:, C_WDWF + 0 * 44 + ch:C_WDWF + 0 * 44 + ch + 1]
                w1 = cols[:, C_WDWF + 1 * 44 + ch:C_WDWF + 1 * 44 + ch + 1]
                w2 = cols[:, C_WDWF + 2 * 44 + ch:C_WDWF + 2 * 44 + ch + 1]
                bb_ = cols[:, C_BDWF + ch:C_BDWF + ch + 1]
                sc.add("act", lambda e, b=b, yi=yi, w2=w2, bb_=bb_: e.activation(
                    out=scr[:, yi, 0:T], in_=ps[:, b, :], func=AF.Identity, scale=w2, bias=bb_),
                    reads=[bank[b], buf("cols")], writes=[yb_])
                sc.add("act", lambda e, b=b, ch=ch: e.activation(
                    out=stash[:, ch, :], in_=ps[:, b, T - 2:T], func=AF.Copy),
                    reads=[bank[b]], writes=[buf("stash"), bank[b]])
                sc.add("dve", lambda e, b=b, yi=yi, w1=w1: e.scalar_tensor_tensor(
                    out=scr[:, yi, 1:T], in0=ps[:, b, 0:T - 1], scalar=w1, in1=scr[:, yi, 1:T],
                    op0=ALU.mult, op1=ALU.add),
                    reads=[bank[b], yb_, buf("cols")], writes=[yb_, bank[b]])
                sc.add("dve", lambda e, b=b, yi=yi, w0=w0: e.scalar_tensor_tensor(
                    out=scr[:, yi, 2:T], in0=ps[:, b, 0:T - 2], scalar=w0, in1=scr[:, yi, 2:T],
                    op0=ALU.mult, op1=ALU.add),
                    reads=[bank[b], yb_, buf("cols")], writes=[yb_])
                if it > 0:
                    sc.add("pool", lambda e, yi=yi, ch=ch: e.tensor_tensor(
                        out=scr[:, yi, 0:2], in0=scr[:, yi, 0:2], in1=carry[:, ch, :], op=ALU.add),
                        reads=[buf("carry"), yb_], writes=[yb_])
            if n > 0:
                ffn_finish(n - 1)
        ffn_finish(NFF - 1)
        if it + 1 < nt:
            to_feature_major(0, 1)
        if it < nt - 1:
            w0t = cols[:, C_WDWF:C_WDWF + 2 * NFF]
            w1t = cols[:, C_WDWF + 2 * NFF:C_WDWF + 4 * NFF]
            sc.add("pool", lambda e: e.tensor_tensor(out=carry[:, :, 0], in0=stash[:, :, 0], in1=w0t, op=ALU.mult),
                   reads=[buf("stash"), buf("cols")], writes=[buf("carry")])
            sc.add("pool", lambda e: e.tensor_tensor(out=ctmp[:, :], in0=stash[:, :, 1], in1=w1t, op=ALU.mult),
                   reads=[buf("stash"), buf("cols")], writes=[buf("ctmp")])
            sc.add("pool", lambda e: e.tensor_tensor(out=carry[:, :, 0], in0=carry[:, :, 0], in1=ctmp[:, :], op=ALU.add),
                   reads=[buf("carry"), buf("ctmp")], writes=[buf("carry")])
            sc.add("pool", lambda e: e.tensor_tensor(out=carry[:, :, 1], in0=stash[:, :, 1], in1=w0t, op=ALU.mult),
                   reads=[buf("stash"), buf("cols")], writes=[buf("carry")])
        for n in range(NFF):
            slot = next_w()
            for s in range(NSUB):
                for h in range(2):
                    b = acc[s * 2 + h]
                    sc.add("pe", lambda e, n=n, s=s, h=h, b=b, slot=slot: e.matmul(
                        ps[:, b, :], lhsT=gT3[:, n, s * P:(s + 1) * P], rhs=ring[:, slot, h * 512:(h + 1) * 512],
                        start=(n == 0), stop=(n == NFF - 1)),
                        reads=[ring_buf[slot], buf("gT%d" % n)], writes=[bank[b]])
        for s in range(NSUB):
            for h in range(2):
                b = acc[s * 2 + h]
                sc.add("dve", lambda e, s=s, h=h, b=b: e.tensor_tensor(
                    out=scr[:, 2 + h, 0:T], in0=ps[:, b, :], in1=g2b[:, h * 512:(h + 1) * 512], op=ALU.mult),
                    reads=[bank[b], buf("gb")], writes=[buf("scr%d" % (2 + h))])
                sc.add("dve", lambda e, s=s, h=h: e.scalar_tensor_tensor(
                    out=xt[:, s, h * 512:(h + 1) * 512], in0=xt[:, s, h * 512:(h + 1) * 512], scalar=ALPHA,
                    in1=scr[:, 2 + h, 0:T], op0=ALU.mult, op1=ALU.add),
                    reads=[xtbs[s], buf("scr%d" % (2 + h))], writes=[xtbs[s]])
        ln_stats(xt, xtbs)
        for s in range(NSUB):
            sc.add("dve", lambda e, s=s: e.scalar_tensor_tensor(
                out=xt[:, s, :], in0=xt[:, s, :], scalar=mv[:, s, 0:1], in1=lnb[:, 2, :],
                op0=ALU.subtract, op1=ALU.mult),
                reads=[xtbs[s], buf("mv"), buf("lnb")], writes=[xtbs[s]])
            sc.add("dve", lambda e, s=s: e.scalar_tensor_tensor(
                out=xt[:, s, :], in0=xt[:, s, :], scalar=rstd4[:, s:s + 1], in1=lnb[:, 3, :],
                op0=ALU.mult, op1=ALU.add),
                reads=[xtbs[s], buf("rstd4"), buf("lnb")], writes=[xtbs[s]])
        sc.add("sp", lambda e, t0=t0: e.dma_start(
            out=out_d[t0:t0 + T, :].rearrange("(s p) d -> p s d", p=P), in_=xt[:, :, :]),
            reads=xtbs, chan="o%d" % (it % 2))


    for it in range(nt):
        do_tile(it)

    sc.finalize()
    handles = {"pe": "tensor", "act": "scalar", "dve": "vector", "pool": "gpsimd", "sp": "sync"}

    def tok(d):
        if d.chan is not None:
            return chan_sem(d.chan), d.chanval
        return sems[d.eng], d.sigval

    for name in sc.chan_cnt:
        chan_sem(name)

    with nc.Block() as block:
        def emit(engname):
            def body(e):
                waited = {}
                for op in sc.ops[engname]:
                    need = []
                    for d in op.waits:
                        sem, val = tok(d)
                        key = id(sem)
                        if waited.get(key, 0) >= val:
                            continue
                        waited[key] = val
                        need = [w for w in need if w[0] is not sem] + [(sem, val)]
                    for (sem, val) in need[:-1]:
                        e.wait_ge(sem, val)
                    ins = op.fn(e)
                    if need:
                        ins._wait_ge(need[-1][0], need[-1][1])
                    if op.chan is not None:
                        ins.then_inc(chan_sem(op.chan), 16)
                    elif op.signal:
                        ins.then_inc(sems[engname], 1)
                if engname == "sp":
                    for name, cnt in sc.chan_cnt.items():
                        e.wait_ge(chan_sem(name), cnt)
            return body

        block.tensor(emit("pe"))
        block.scalar(emit("act"))
        block.vector(emit("dve"))
        block.gpsimd(emit("pool"))
        block.sync(emit("sp"))
    es.close()
    return nc


def _colmajor(v, nchunk):
    return np.ascontiguousarray(v.reshape(nchunk, P).T)


def _prep_shared(w_ada, b_ada, w_in, w_dw_a, b_dw_a, ln_a_g, ln_a_b, w_pw_a, w_pool, pool_scale, w_out,
                 ln1_g, ln1_b, w_up, w_dw_f, b_dw_f, w_down, ln2_g, ln2_b):
    f = np.float32
    cols = np.zeros((P, NCOLS), f)
    cols[:, C_BDWA:C_BDWA + 4] = _colmajor(b_dw_a[0], 4)
    cols[:, C_LNAG:C_LNAG + 4] = _colmajor(ln_a_g[0], 4)
    cols[:, C_LNAB:C_LNAB + 4] = _colmajor(ln_a_b[0], 4)
    cols[:, C_PSC:C_PSC + 8] = _colmajor(pool_scale[0], 8)
    cols[:, C_WDWA:C_WDWA + 124] = w_dw_a[0].reshape(CONV_K, 4, P).transpose(2, 0, 1).reshape(P, 124)
    cols[:, C_WDWF:C_WDWF + 132] = w_dw_f[0].reshape(3, 44, P).transpose(2, 0, 1).reshape(P, 132)
    cols[:, C_BDWF:C_BDWF + 44] = _colmajor(b_dw_f[0], 44)
    rc = np.zeros((4, 16), f)
    for g in range(4):
        w = 2 << g
        for t in range(w - 1):
            rc[g, t] = 1.0 / float(t + 1)
    cols[:, C_RCNT:C_RCNT + 64] = np.broadcast_to(rc.reshape(1, 64), (P, 64))

    def chunk_cols(wm, n):
        return wm[:, n * P:(n + 1) * P].reshape(KC, P, P).transpose(1, 0, 2).reshape(P, KC * P)

    win = w_in[0]
    wup = w_up[0]
    chunks = []
    for n in range(4):
        chunks.append(chunk_cols(win, n))
        chunks.append(chunk_cols(win, 4 + n))
    for n in range(8, 12):
        chunks.append(chunk_cols(win, n))
    for j in range(8):
        chunks.append(chunk_cols(win, 12 + j))
        chunks.append(chunk_cols(win, 20 + j))
    for kc in range(8):
        chunks.append(w_out[0][kc * P:(kc + 1) * P, :])
    for n in range(NFF):
        chunks.append(chunk_cols(wup, n))
        chunks.append(chunk_cols(wup, NFF + n))
    for n in range(NFF):
        chunks.append(w_down[0][n * P:(n + 1) * P, :])
    wstream = np.ascontiguousarray(np.concatenate(chunks, axis=0)).astype(f, copy=False)
    assert wstream.shape == (NCHUNK_TILE * P, D)
    wada = np.ascontiguousarray(
        w_ada[0].reshape(KC, P, 6, D).transpose(1, 2, 0, 3).reshape(P, 6, KC * D))
    bada = np.ascontiguousarray(b_ada[0].reshape(6, D))
    wpw = np.ascontiguousarray(w_pw_a[0].reshape(4, P, D).transpose(1, 0, 2).reshape(P, 4 * D))
    wpool = np.ascontiguousarray(w_pool[0].transpose(1, 0, 2).reshape(P, 4 * 256))
    lnv = np.ascontiguousarray(np.stack([ln1_g[0], ln1_b[0], ln2_g[0], ln2_b[0]], axis=0))
    return dict(cols=cols, wada=wada, bada=bada, wstream=wstream, wpw=wpw, wpool=wpool, lnv=lnv,
                ident=np.eye(P, dtype=f))


def kernel(x, c, w_ada, b_ada, w_in, w_dw_a, b_dw_a, ln_a_g, ln_a_b, w_pw_a, w_pool, pool_scale, w_out,
           ln1_g, ln1_b, w_up, w_dw_f, b_dw_f, w_down, ln2_g, ln2_b):
    args = [np.asarray(a, dtype=np.float32) for a in (
        w_ada, b_ada, w_in, w_dw_a, b_dw_a, ln_a_g, ln_a_b, w_pw_a, w_pool, pool_scale, w_out,
        ln1_g, ln1_b, w_up, w_dw_f, b_dw_f, w_down, ln2_g, ln2_b)]
    x = np.asarray(x, dtype=np.float32)
    c = np.asarray(c, dtype=np.float32)
    shared = _prep_shared(*args)
    nb = x.shape[0]
    in_maps = []
    for b in range(nb):
        m = dict(shared)
        cols = shared["cols"].copy()
        cols[:, C_C:C_C + KC] = _colmajor(c[b], KC)
        m["cols"] = cols
        m["x"] = np.ascontiguousarray(x[b])
        in_maps.append(m)
    nc = build_nc()
    res = run_bass_kernel_spmd(nc, in_maps, core_ids=list(range(nb)))
    out = np.stack([np.asarray(r["out"]) for r in res.results], axis=0)
    return out.astype(np.float32, copy=False)
```

```python
import numpy as np
from contextlib import ExitStack
import concourse.bass as bass
import concourse.mybir as mybir
from concourse.bass_utils import run_bass_kernel_spmd

F32 = mybir.dt.float32
BF16 = mybir.dt.bfloat16
AF = mybir.ActivationFunctionType
ALU = mybir.AluOpType

P = 128
D = 1024
S = 4096
T = 512
NT = S // T
NSUB = T // P
KC = D // P
CONV_K = 31
DFF = 2816
NFF = DFF // P
ALPHA = 2.0 ** 0.25
EPS = 1e-5
NRING = 7
NCHUNK_TILE = 12 + 16 + 8 + 44 + 22

C_BDWA = 0
C_LNAG = 4
C_LNAB = 8
C_PSC = 12
C_WDWA = 20
C_WDWF = 144
C_BDWF = 276
C_C = 320
C_RCNT = 328
NCOLS = 392


class Buf:
    __slots__ = ("name", "w", "rs")

    def __init__(self, name):
        self.name = name
        self.w = None
        self.rs = []


class Op:
    __slots__ = ("eng", "fn", "waits", "signal", "sigval", "chan", "chanval")

    def __init__(self, eng, fn):
        self.eng = eng
        self.fn = fn
        self.waits = []
        self.signal = False
        self.sigval = 0
        self.chan = None
        self.chanval = 0


ENGS = ("pe", "act", "dve", "pool", "sp")


class Sched:
    def __init__(self):
        self.ops = {e: [] for e in ENGS}
        self.chan_cnt = {}

    def add(self, eng, fn, reads=(), writes=(), chan=None):
        op = Op(eng, fn)
        if chan is not None:
            op.chan = chan
            self.chan_cnt[chan] = self.chan_cnt.get(chan, 0) + 16
            op.chanval = self.chan_cnt[chan]
        deps = []
        for b in reads:
            if b.w is not None:
                deps.append((b.w, True))
        for b in writes:
            if b.w is not None:
                deps.append((b.w, False))
            last = {}
            for r in b.rs:
                last[r.chan if r.chan is not None else r.eng] = r
            for r in last.values():
                deps.append((r, False))
        seen = set()
        for (d, raw) in deps:
            if id(d) in seen:
                continue
            if d.chan is None and op.chan is None and d.eng == eng:
                if eng == "pe":
                    continue
            seen.add(id(d))
            if d.chan is None:
                d.signal = True
            op.waits.append(d)
        for b in reads:
            b.rs.append(op)
        for b in writes:
            b.w = op
            b.rs = []
        self.ops[eng].append(op)
        return op

    def finalize(self):
        for e in ENGS:
            n = 0
            for op in self.ops[e]:
                if op.chan is None and op.signal:
                    n += 1
                    op.sigval = n


def build_nc(nt=NT):
    S = nt * T
    nc = bass.Bass("TRN2", target_bir_lowering=False)

    def din(name, shape, dt=F32):
        return nc.dram_tensor(name, list(shape), dt, kind="ExternalInput").ap()

    x_d = din("x", [S, D])
    cols_d = din("cols", [P, NCOLS])
    ident_d = din("ident", [P, P])
    wada_d = din("wada", [P, 6, KC * D])
    bada_d = din("bada", [6, D])
    wst_d = din("wstream", [NCHUNK_TILE * P, D])
    wpw_d = din("wpw", [P, 4 * D])
    wpool_d = din("wpool", [P, D])
    lnv_d = din("lnv", [4, D])
    out_d = nc.dram_tensor("out", [S, D], F32, kind="ExternalOutput").ap()
    wsc_d = nc.dram_tensor("wsc", [NCHUNK_TILE * P, D], BF16, kind="Internal").ap()

    sc = Sched()
    es = ExitStack()

    def sb(name, shape, dt=F32):
        return es.enter_context(nc.sbuf_tensor("sb_" + name, list(shape), dt))

    cols = sb("cols", [P, NCOLS])
    identf = sb("identf", [P, P])
    identb = sb("identb", [P, P], BF16)
    ones_f = sb("ones_f", [P, P])
    ones_row = sb("ones_row", [1, P])
    eps_t = sb("eps_t", [P, 1])
    mhalf_t = sb("mhalf_t", [P, 1])
    diag = sb("diag", [P, CONV_K * 4, P], BF16)
    g1b = sb("g1b", [P, D])
    g2b = sb("g2b", [P, D])
    lnb = sb("lnb", [P, 4, D])
    modcols = sb("modcols", [P, 4, KC])
    wpw = sb("wpw", [P, 4, D], BF16)
    wpool = sb("wpool", [P, D], BF16)
    s_bf = sb("s_bf", [P, KC], BF16)
    xt_all = sb("xt", [P, 2, NSUB, D])
    xn = sb("xn", [P, NSUB, D], BF16)
    hT = sb("hT", [P, KC, T + 2], BF16)
    glu = sb("glu", [P, 4, T + 30], BF16)
    ybuf = sb("ybuf", [P, 4, T])
    mrow = ybuf[0:1, 0:2, :].rearrange("p a b -> p (a b)")
    brow = ybuf[0:1, 2:4, :].rearrange("p a b -> p (a b)")
    scr = sb("scr", [P, 8, T + 16])
    zT = sb("zT", [P, 4, T], BF16)
    ubuf = sb("ubuf", [P, 4, T + 15])
    pooledT = sb("pooledT", [P, 4, T], BF16)
    merged = sb("merged", [P, KC, T], BF16)
    gT = sb("gT", [P, NFF * T], BF16)
    stash = sb("stash", [P, 2 * NFF, 2])
    carry = sb("carry", [P, 2 * NFF, 2])
    ctmp = sb("ctmp", [P, 2 * NFF])
    ring = sb("ring", [P, NRING, D], BF16)
    st = sb("st", [P, NSUB, 2, 6])
    mv = sb("mv", [P, NSUB, 2])
    sums = sb("sums", [P, NSUB, 2])
    sd4 = sb("sd4", [P, NSUB])
    rstd4 = sb("rstd4", [P, NSUB])
    nb4 = sb("nb4", [P, NSUB])
    ps = es.enter_context(nc.psum_tensor("ps", [P, 8, 512], F32))

    sems = {e: es.enter_context(nc.semaphore("sem_" + e)) for e in ENGS}
    chan_sems = {}

    def chan_sem(name):
        if name not in chan_sems:
            chan_sems[name] = es.enter_context(nc.semaphore("ch_" + name))
        return chan_sems[name]

    B = {}

    def buf(name):
        if name not in B:
            B[name] = Buf(name)
        return B[name]

    bank = [buf("bank%d" % i) for i in range(8)]
    bank_rr = [0]

    def next_bank():
        b = bank_rr[0]
        bank_rr[0] = (b + 1) % 8
        return b

    gT3 = gT[:, :].rearrange("p (n t) -> p n t", n=NFF)
    stg = gT[:, 0:KC * D].rearrange("p (k n) -> p k n", k=KC)
    stgb = [buf("gT%d" % q) for q in range(KC * D // T)]
    glub = [buf("glu%d" % q) for q in range(4)]
    ubufb = [buf("ubuf%d" % q) for q in range(4)]

    sc.add("sp", lambda e: e.dma_start(out=cols[:, :], in_=cols_d), writes=[buf("cols")], chan="su0")
    sc.add("sp", lambda e: e.dma_start(out=identf[:, :], in_=ident_d), writes=[buf("identf")], chan="su1")
    for i in range(4):
        sc.add("sp", lambda e, i=i: e.dma_start(out=lnb[:, i, :], in_=lnv_d[i:i + 1, :].broadcast_to([P, D])),
               writes=[buf("lnb")], chan="su2")
    sc.add("pool", lambda e: e.memset(ones_f[:, :], 1.0 / 512.0), writes=[buf("ones_f")])
    sc.add("pool", lambda e: e.memset(ones_row[:, :], 1.0), writes=[buf("ones_row")])
    sc.add("pool", lambda e: e.memset(eps_t[:, :], EPS), writes=[buf("eps_t")])
    sc.add("pool", lambda e: e.memset(mhalf_t[:, :], -0.5), writes=[buf("mhalf_t")])
    sc.add("pool", lambda e: e.dma_start(out=wpw[:, :, :], in_=wpw_d.rearrange("p (c d) -> p c d", c=4)),
           writes=[buf("wpw")], chan="sg0")
    sc.add("pool", lambda e: e.dma_start(out=wpool[:, :], in_=wpool_d), writes=[buf("wpool")], chan="sg1")
    PIECE_CH = 3
    NPIECE = NCHUNK_TILE // PIECE_CH
    wsc_piece = [buf("wscp%d" % i) for i in range(NPIECE)]
    cast_state = {"n": 0}

    def cast_piece(_i, count=6):
        for _ in range(count):
            i = cast_state["n"]
            if i >= NPIECE:
                return
            cast_state["n"] = i + 1
            r0 = i * PIECE_CH * P
            r1 = (i + 1) * PIECE_CH * P
            sc.add("pool", lambda e, r0=r0, r1=r1: e.dma_start(out=wsc_d[r0:r1, :], in_=wst_d[r0:r1, :]),
                   writes=[wsc_piece[i]], chan="wc%d" % i)

    sc.add("act", lambda e: e.activation(out=identb[:, :], in_=identf[:, :], func=AF.Copy),
           reads=[buf("identf")], writes=[buf("identb")])
    sc.add("act", lambda e: e.activation(out=s_bf[:, :], in_=cols[:, C_C:C_C + KC], func=AF.Silu),
           reads=[buf("cols")], writes=[buf("s_bf")])

    def mod_issue(p):
        sc.add("pool", lambda e: e.dma_start(out=stg, in_=wada_d[:, p, :].rearrange("p (k n) -> p k n", k=KC)),
               writes=stgb, chan="wada")

    def mod_consume(p):
        sc.add("sp", lambda e: e.dma_start(out=brow, in_=bada_d[p:p + 1, :]), writes=[buf("ybuf2"), buf("ybuf3")], chan="su3")
        for h in range(2):
            b = next_bank()
            for kc in range(KC):
                sc.add("pe", lambda e, b=b, kc=kc, h=h: e.matmul(
                    ps[0:1, b, :], lhsT=s_bf[:, kc:kc + 1], rhs=stg[:, kc, h * 512:(h + 1) * 512],
                    start=(kc == 0), stop=(kc == KC - 1)),
                    reads=[buf("s_bf")] + stgb, writes=[bank[b]])
            sc.add("dve", lambda e, b=b, h=h: e.tensor_tensor(
                out=mrow[:, h * 512:(h + 1) * 512], in0=ps[0:1, b, :], in1=brow[:, h * 512:(h + 1) * 512],
                op=ALU.add), reads=[bank[b], buf("ybuf2"), buf("ybuf3")], writes=[buf("ybuf0"), buf("ybuf1")])
        if p in (2, 5):
            dst = g1b if p == 2 else g2b
            for h in range(2):
                b = next_bank()
                sc.add("pe", lambda e, b=b, h=h: e.matmul(
                    ps[:, b, :], lhsT=ones_row[0:1, :], rhs=mrow[:, h * 512:(h + 1) * 512], start=True, stop=True),
                    reads=[buf("ones_row"), buf("ybuf0"), buf("ybuf1")], writes=[bank[b]])
                sc.add("act", lambda e, b=b, h=h, dst=dst: e.activation(
                    out=dst[:, h * 512:(h + 1) * 512], in_=ps[:, b, :], func=AF.Copy),
                    reads=[bank[b]], writes=[buf("gb")])
        else:
            idx = {0: 1, 1: 0, 3: 3, 4: 2}[p]
            b = next_bank()
            for kc in range(KC):
                sc.add("pe", lambda e, b=b, kc=kc: e.matmul(
                    ps[:, b, kc:kc + 1], lhsT=mrow[:, kc * P:(kc + 1) * P], rhs=ones_row[0:1, 0:1],
                    start=True, stop=True),
                    reads=[buf("ones_row"), buf("ybuf0"), buf("ybuf1")], writes=[bank[b]])
            addc = 1.0 if p in (1, 4) else 0.0
            sc.add("dve", lambda e, b=b, idx=idx, addc=addc: e.tensor_scalar(
                out=modcols[:, idx, :], in0=ps[:, b, 0:KC], scalar1=addc, scalar2=None, op0=ALU.add),
                reads=[bank[b]], writes=[buf("modcols")])

    mod_issue(0)
    cast_piece(0, 4)
    for j in range(CONV_K):
        for c in range(4):
            i = j * 4 + c
            sc.add("act", lambda e, i=i: e.activation(
                out=diag[:, i, :], in_=identf[:, :], func=AF.Copy, scale=cols[:, C_WDWA + i:C_WDWA + i + 1]),
                reads=[buf("identf"), buf("cols")], writes=[buf("diag")])

    mod_consume(0)
    mod_issue(1)
    cast_piece(0, 8)
    mod_consume(1)
    mod_issue(2)
    cast_piece(0, 22)

    ring_state = {"n": 0}
    ring_buf = [buf("ring%d" % i) for i in range(NRING)]

    def ring_load():
        n = ring_state["n"]
        ring_state["n"] = n + 1
        slot = n % NRING
        k = n % NCHUNK_TILE
        piece = wsc_piece[k // PIECE_CH]
        sc.add("sp", lambda e: e.dma_start(out=ring[:, slot, :], in_=wsc_d[k * P:(k + 1) * P, :]),
               reads=[piece], writes=[ring_buf[slot]], chan="ring%d" % slot)
        return slot

    pending = []
    PREFETCH = 5
    total_chunks = NCHUNK_TILE * nt

    def next_w():
        while len(pending) < PREFETCH + 1 and ring_state["n"] < total_chunks:
            pending.append(ring_load())
        return pending.pop(0)

    def ln_stats(xt, xtbs):
        for s in range(NSUB):
            sc.add("act", lambda e, s=s: e.activation(out=xn[:, s, :], in_=xt[:, s, :], func=AF.Copy,
                                                       accum_out=sums[:, s, 0:1]),
                   reads=[xtbs[s]], writes=[buf("xn"), buf("sums")])
            sc.add("act", lambda e, s=s: e.activation(out=xn[:, s, :], in_=xt[:, s, :], func=AF.Square,
                                                       accum_out=sums[:, s, 1:2]),
                   reads=[xtbs[s]], writes=[buf("xn"), buf("sums")])
        sc.add("act", lambda e: e.activation(out=mv[:, :, 0], in_=sums[:, :, 0], func=AF.Copy, scale=1.0 / D),
               reads=[buf("sums")], writes=[buf("mv")])
        sc.add("dve", lambda e: e.tensor_tensor(out=sd4[:, :], in0=mv[:, :, 0], in1=mv[:, :, 0], op=ALU.mult),
               reads=[buf("mv")], writes=[buf("sd4")])
        sc.add("dve", lambda e: e.scalar_tensor_tensor(
            out=mv[:, :, 1], in0=sums[:, :, 1], scalar=1.0 / D, in1=sd4[:, :], op0=ALU.mult, op1=ALU.subtract),
            reads=[buf("sums"), buf("sd4")], writes=[buf("mv")])
        sc.add("act", lambda e: e.activation(out=sd4[:, :], in_=mv[:, :, 1], func=AF.Ln,
                                              bias=eps_t[:, 0:1], scale=1.0),
               reads=[buf("mv"), buf("eps_t")], writes=[buf("sd4")])
        sc.add("act", lambda e: e.activation(out=rstd4[:, :], in_=sd4[:, :], func=AF.Exp, scale=-0.5),
               reads=[buf("sd4")], writes=[buf("rstd4")])
        sc.add("dve", lambda e: e.scalar_tensor_tensor(
            out=nb4[:, :], in0=mv[:, :, 0], scalar=-1.0, in1=rstd4[:, :], op0=ALU.mult, op1=ALU.mult),
            reads=[buf("mv"), buf("rstd4")], writes=[buf("nb4")])

    def to_feature_major(mod_scale_idx, mod_shift_idx):
        for kc in range(KC):
            b = next_bank()
            pb = ps[:, b, :].bitcast(BF16)
            for s in range(NSUB):
                sc.add("pe", lambda e, pb=pb, s=s, kc=kc: e.transpose(
                    pb[:, s * P:(s + 1) * P], xn[:, s, kc * P:(kc + 1) * P], identb[:, :]),
                    reads=[buf("xn"), buf("identb")], writes=[bank[b]])
            eng = "act" if kc % 2 == 0 else "dve"
            if eng == "act":
                sc.add("act", lambda e, pb=pb, kc=kc: e.activation(
                    out=hT[:, kc, 2:2 + T], in_=pb[:, 0:T], func=AF.Identity,
                    scale=modcols[:, mod_scale_idx, kc:kc + 1], bias=modcols[:, mod_shift_idx, kc:kc + 1]),
                    reads=[bank[b], buf("modcols")], writes=[buf("hT%d" % kc)])
            else:
                sc.add("dve", lambda e, pb=pb, kc=kc: e.tensor_scalar(
                    out=hT[:, kc, 2:2 + T], in0=pb[:, 0:T],
                    scalar1=modcols[:, mod_scale_idx, kc:kc + 1], scalar2=modcols[:, mod_shift_idx, kc:kc + 1],
                    op0=ALU.mult, op1=ALU.add),
                    reads=[bank[b], buf("modcols")], writes=[buf("hT%d" % kc)])

    hT_bufs = [buf("hT%d" % kc) for kc in range(KC)]

    def proj_chunk(b, slot, rhs_lo=2):
        for kc in range(KC):
            sc.add("pe", lambda e, kc=kc: e.matmul(
                ps[:, b, :], lhsT=ring[:, slot, kc * P:(kc + 1) * P], rhs=hT[:, kc, rhs_lo:rhs_lo + T],
                start=(kc == 0), stop=(kc == KC - 1)),
                reads=[ring_buf[slot], hT_bufs[kc]], writes=[bank[b]])

    def front_ln(it):
        t0 = it * T
        xt = xt_all[:, it % 2]
        xtbs = [buf("xt%d_%d" % (it % 2, q)) for q in range(NSUB)]
        sc.add("sp", lambda e, t0=t0: e.dma_start(
            out=xt[:, :, :], in_=x_d[t0:t0 + T, :].rearrange("(s p) d -> p s d", p=P)),
            writes=xtbs, chan="x%d" % (it % 2))
        ln_stats(xt, xtbs)
        for s in range(NSUB):
            sc.add("act", lambda e, s=s: e.activation(
                out=xn[:, s, :], in_=xt[:, s, :], func=AF.Identity, scale=rstd4[:, s:s + 1], bias=nb4[:, s:s + 1]),
                reads=[xtbs[s], buf("rstd4"), buf("nb4")], writes=[buf("xn")])

    def do_tile(it):
        t0 = it * T
        xt = xt_all[:, it % 2]
        xtbs = [buf("xt%d_%d" % (it % 2, q)) for q in range(NSUB)]
        if it == 0:
            front_ln(0)
            to_feature_major(0, 1)

        if it == 0:
            sc.add("pool", lambda e: e.memset(glu[:, :, 0:30], 0.0), writes=glub)
            sc.add("pool", lambda e: e.memset(ubuf[:, :, 0:15], 0.0), writes=ubufb)
        else:
            sc.add("pool", lambda e: e.tensor_copy(out=glu[:, :, 0:30], in_=glu[:, :, T:T + 30]),
                   reads=glub, writes=glub)
            sc.add("pool", lambda e: e.tensor_copy(out=ubuf[:, :, 0:15], in_=ubuf[:, :, T:T + 15]),
                   reads=ubufb, writes=ubufb)

        for c in range(4):
            bv = next_bank()
            proj_chunk(bv, next_w())
            bg = next_bank()
            proj_chunk(bg, next_w())
            sc.add("act", lambda e, bg=bg, c=c: e.activation(out=scr[:, c % 2, 0:T], in_=ps[:, bg, :], func=AF.Sigmoid),
                   reads=[bank[bg]], writes=[buf("scr%d" % (c % 2))])
            sc.add("dve", lambda e, bv=bv, c=c: e.tensor_tensor(
                out=glu[:, c, 30:30 + T], in0=ps[:, bv, :], in1=scr[:, c % 2, 0:T], op=ALU.mult),
                reads=[bank[bv], buf("scr%d" % (c % 2))], writes=[glub[c]])
        for g in range(4):
            slot = next_w()
            b = next_bank()
            proj_chunk(b, slot)
            sc.add("act", lambda e, b=b, g=g: e.activation(out=ubuf[:, g, 15:15 + T], in_=ps[:, b, :], func=AF.Copy),
                   reads=[bank[b]], writes=[ubufb[g]])
        if it == 0:
            mod_consume(2)
            mod_issue(3)
        for c in range(4):
            b = next_bank()
            for j in range(CONV_K):
                sc.add("pe", lambda e, b=b, c=c, j=j: e.matmul(
                    ps[:, b, :], lhsT=diag[:, j * 4 + c, :], rhs=glu[:, c, j:j + T],
                    start=(j == 0), stop=(j == CONV_K - 1)),
                    reads=[buf("diag"), glub[c]], writes=[bank[b]])
            sc.add("act", lambda e, b=b, c=c: e.activation(
                out=ybuf[:, c, :], in_=ps[:, b, :], func=AF.Identity, bias=cols[:, C_BDWA + c:C_BDWA + c + 1], scale=1.0),
                reads=[bank[b], buf("cols")], writes=[buf("ybuf%d" % c)])
            sc.add("act", lambda e, b=b, c=c: e.activation(
                out=scr[:, c % 2, 0:T], in_=ps[:, b, :], func=AF.Square, bias=cols[:, C_BDWA + c:C_BDWA + c + 1], scale=1.0),
                reads=[bank[b], buf("cols")], writes=[buf("scr%d" % (c % 2))])
            if c == 0:
                bm = next_bank()
                bq = next_bank()
            sc.add("pe", lambda e, c=c, bm=bm: e.matmul(
                ps[:, bm, :], lhsT=ones_f[:, :], rhs=ybuf[:, c, :], start=(c == 0), stop=(c == 3)),
                reads=[buf("ones_f"), buf("ybuf%d" % c)], writes=[bank[bm]])
            sc.add("pe", lambda e, c=c, bq=bq: e.matmul(
                ps[:, bq, :], lhsT=ones_f[:, :], rhs=scr[:, c % 2, 0:T], start=(c == 0), stop=(c == 3)),
                reads=[buf("ones_f"), buf("scr%d" % (c % 2))], writes=[bank[bq]])
        sc.add("act", lambda e, bm=bm: e.activation(out=scr[:, 4, 0:T], in_=ps[:, bm, :], func=AF.Copy),
               reads=[bank[bm]], writes=[buf("scr4")])
        sc.add("act", lambda e, bm=bm: e.activation(out=scr[:, 2, 0:T], in_=ps[:, bm, :], func=AF.Square),
               reads=[bank[bm]], writes=[buf("scr2")])
        sc.add("dve", lambda e, bq=bq: e.tensor_tensor(out=scr[:, 3, 0:T], in0=ps[:, bq, :], in1=scr[:, 2, 0:T],
                                                        op=ALU.subtract),
               reads=[bank[bq], buf("scr2")], writes=[buf("scr3")])
        sc.add("act", lambda e: e.activation(out=scr[:, 2, 0:T], in_=scr[:, 3, 0:T], func=AF.Ln,
                                              bias=eps_t[:, 0:1], scale=1.0),
               reads=[buf("scr3"), buf("eps_t")], writes=[buf("scr2")])
        sc.add("act", lambda e: e.activation(out=scr[:, 5, 0:T], in_=scr[:, 2, 0:T], func=AF.Exp, scale=-0.5),
               reads=[buf("scr2")], writes=[buf("scr5")])
        for c in range(4):
            sc.add("dve", lambda e, c=c: e.tensor_tensor(out=scr[:, 2 + c % 2, 0:T], in0=ybuf[:, c, :], in1=scr[:, 4, 0:T],
                                                          op=ALU.subtract),
                   reads=[buf("ybuf%d" % c), buf("scr4")], writes=[buf("scr%d" % (2 + c % 2))])
            sc.add("dve", lambda e, c=c: e.tensor_tensor(out=scr[:, 2 + c % 2, 0:T], in0=scr[:, 2 + c % 2, 0:T], in1=scr[:, 5, 0:T],
                                                          op=ALU.mult),
                   reads=[buf("scr%d" % (2 + c % 2)), buf("scr5")], writes=[buf("scr%d" % (2 + c % 2))])
            sc.add("act", lambda e, c=c: e.activation(
                out=zT[:, c, :], in_=scr[:, 2 + c % 2, 0:T], func=AF.Silu,
                scale=cols[:, C_LNAG + c:C_LNAG + c + 1], bias=cols[:, C_LNAB + c:C_LNAB + c + 1]),
                reads=[buf("scr%d" % (2 + c % 2)), buf("cols")], writes=[buf("zT%d" % c)])
        for g in range(4):
            w = 2 << g
            src = None
            for k in range(g + 1):
                sh = 1 << k
                lo = (2 << k) - 1
                dst = scr[:, 6 + k % 2, 0:T + 15]
                a = ubuf[:, g, :] if k == 0 else scr[:, 6 + (k - 1) % 2, 0:T + 15]
                rd = [ubufb[g]] if k == 0 else [buf("scr%d" % (6 + (k - 1) % 2))]
                sc.add("pool", lambda e, dst=dst, a=a, lo=lo, sh=sh: e.tensor_tensor(
                    out=dst[:, lo:T + 15], in0=a[:, lo:T + 15], in1=a[:, lo - sh:T + 15 - sh], op=ALU.add),
                    reads=rd, writes=[buf("scr%d" % (6 + k % 2))])
                src = (dst, buf("scr%d" % (6 + k % 2)))
            sc.add("dve", lambda e, g=g, w=w, src=src: e.scalar_tensor_tensor(
                out=pooledT[:, g, :], in0=src[0][:, 15:15 + T], scalar=1.0 / w, in1=ubuf[:, g, 15:15 + T],
                op0=ALU.mult, op1=ALU.subtract),
                reads=[src[1], ubufb[g]], writes=[buf("pooledT%d" % g)])
            if it == 0:
                n = w - 1
                sc.add("pool", lambda e, g=g, n=n, src=src: e.tensor_tensor(
                    out=src[0][:, 15:15 + n], in0=src[0][:, 15:15 + n],
                    in1=cols[:, C_RCNT + g * 16:C_RCNT + g * 16 + n], op=ALU.mult),
                    reads=[src[1], buf("cols")], writes=[src[1]])
                sc.add("pool", lambda e, g=g, n=n, src=src: e.tensor_tensor(
                    out=pooledT[:, g, 0:n], in0=src[0][:, 15:15 + n], in1=ubuf[:, g, 15:15 + n], op=ALU.subtract),
                    reads=[src[1], ubufb[g]], writes=[buf("pooledT%d" % g)])
        for jj in range(0, KC, 2):
            gate_banks = {}
            for j in (jj, jj + 1):
                sa_slot = next_w()
                ba = next_bank()
                proj_chunk(ba, sa_slot)
                sb_slot = next_w()
                bb = next_bank()
                proj_chunk(bb, sb_slot)
                gate_banks[j] = (ba, bb)
            for j in (jj, jj + 1):
                ba, bb = gate_banks[j]
                q = 4 * (j % 2)
                bc = next_bank()
                for c in range(4):
                    sc.add("pe", lambda e, c=c, j=j, bc=bc: e.matmul(
                        ps[:, bc, :], lhsT=wpw[:, c, j * P:(j + 1) * P], rhs=zT[:, c, :], start=(c == 0), stop=(c == 3)),
                        reads=[buf("wpw"), buf("zT%d" % c)], writes=[bank[bc]])
                bd = next_bank()
                g = j // 2
                sc.add("pe", lambda e, j=j, g=g, bd=bd: e.matmul(
                    ps[:, bd, :], lhsT=wpool[:, g * 256 + (j % 2) * P:g * 256 + (j % 2) * P + P], rhs=pooledT[:, g, :],
                    start=True, stop=True),
                    reads=[buf("wpool"), buf("pooledT%d" % g)], writes=[bank[bd]])
                sc.add("act", lambda e, ba=ba, q=q: e.activation(out=scr[:, q + 0, 0:T], in_=ps[:, ba, :], func=AF.Sigmoid),
                       reads=[bank[ba]], writes=[buf("scr%d" % (q + 0))])
                sc.add("act", lambda e, bb=bb, q=q: e.activation(out=scr[:, q + 1, 0:T], in_=ps[:, bb, :], func=AF.Sigmoid),
                       reads=[bank[bb]], writes=[buf("scr%d" % (q + 1))])
                sc.add("dve", lambda e, bc=bc, q=q: e.tensor_tensor(out=scr[:, q + 2, 0:T], in0=ps[:, bc, :],
                                                                      in1=scr[:, q + 0, 0:T], op=ALU.mult),
                       reads=[bank[bc], buf("scr%d" % (q + 0))], writes=[buf("scr%d" % (q + 2))])
                sc.add("dve", lambda e, bd=bd, j=j, q=q: e.scalar_tensor_tensor(
                    out=scr[:, q + 3, 0:T], in0=ps[:, bd, :], scalar=cols[:, C_PSC + j:C_PSC + j + 1],
                    in1=scr[:, q + 1, 0:T], op0=ALU.mult, op1=ALU.mult),
                    reads=[bank[bd], buf("scr%d" % (q + 1)), buf("cols")], writes=[buf("scr%d" % (q + 3))])
                sc.add("pool", lambda e, j=j, q=q: e.tensor_tensor(out=merged[:, j, :], in0=scr[:, q + 2, 0:T],
                                                                    in1=scr[:, q + 3, 0:T], op=ALU.add),
                       reads=[buf("scr%d" % (q + 2)), buf("scr%d" % (q + 3))], writes=[buf("merged%d" % j)])
        if it == 0:
            mod_consume(3)
            mod_issue(4)
        acc = list(range(8))
        for kc in range(KC):
            slot = next_w()
            for s in range(NSUB):
                for h in range(2):
                    b = acc[s * 2 + h]
                    sc.add("pe", lambda e, kc=kc, s=s, h=h, b=b, slot=slot: e.matmul(
                        ps[:, b, :], lhsT=merged[:, kc, s * P:(s + 1) * P], rhs=ring[:, slot, h * 512:(h + 1) * 512],
                        start=(kc == 0), stop=(kc == KC - 1)),
                        reads=[ring_buf[slot], buf("merged%d" % kc)], writes=[bank[b]])
        for s in range(NSUB):
            for h in range(2):
                b = acc[s * 2 + h]
                sc.add("dve", lambda e, s=s, h=h, b=b: e.tensor_tensor(
                    out=scr[:, 2 + h, 0:T], in0=ps[:, b, :], in1=g1b[:, h * 512:(h + 1) * 512], op=ALU.mult),
                    reads=[bank[b], buf("gb")], writes=[buf("scr%d" % (2 + h))])
                sc.add("dve", lambda e, s=s, h=h: e.scalar_tensor_tensor(
                    out=xt[:, s, h * 512:(h + 1) * 512], in0=xt[:, s, h * 512:(h + 1) * 512], scalar=ALPHA,
                    in1=scr[:, 2 + h, 0:T], op0=ALU.mult, op1=ALU.add),
                    reads=[xtbs[s], buf("scr%d" % (2 + h))], writes=[xtbs[s]])
        if it == 0:
            mod_consume(4)
            mod_issue(5)
        ln_stats(xt, xtbs)
        for s in range(NSUB):
            sc.add("dve", lambda e, s=s: e.scalar_tensor_tensor(
                out=xt[:, s, :], in0=xt[:, s, :], scalar=mv[:, s, 0:1], in1=lnb[:, 0, :],
                op0=ALU.subtract, op1=ALU.mult),
                reads=[xtbs[s], buf("mv"), buf("lnb")], writes=[xtbs[s]])
            sc.add("dve", lambda e, s=s: e.scalar_tensor_tensor(
                out=xt[:, s, :], in0=xt[:, s, :], scalar=rstd4[:, s:s + 1], in1=lnb[:, 1, :],
                op0=ALU.mult, op1=ALU.add),
                reads=[xtbs[s], buf("rstd4"), buf("lnb")], writes=[xtbs[s]])
        if it == 0:
            sc.add("pool", lambda e: e.memset(hT[:, :, 0:2], 0.0), writes=hT_bufs)
        ln_stats(xt, xtbs)
        for s in range(NSUB):
            sc.add("act", lambda e, s=s: e.activation(
                out=xn[:, s, :], in_=xt[:, s, :], func=AF.Identity, scale=rstd4[:, s:s + 1], bias=nb4[:, s:s + 1]),
                reads=[xtbs[s], buf("rstd4"), buf("nb4")], writes=[buf("xn")])
        to_feature_major(2, 3)

        if it == 0:
            mod_consume(5)
        YV = (0, 1, 7)
        YG = (2, 3, 6)

        def ffn_finish(n):
            yv = YV[n % 3]
            yg = YG[n % 3]
            sc.add("act", lambda e, n=n, yg=yg: e.activation(out=scr[:, 4 + n % 2, 0:T], in_=scr[:, yg, 0:T],
                                                               func=AF.Gelu_apprx_tanh),
                   reads=[buf("scr%d" % yg)], writes=[buf("scr%d" % (4 + n % 2))])
            sc.add("pool", lambda e, n=n, yv=yv: e.tensor_tensor(
                out=gT3[:, n, :], in0=scr[:, 4 + n % 2, 0:T], in1=scr[:, yv, 0:T], op=ALU.mult),
                reads=[buf("scr%d" % (4 + n % 2)), buf("scr%d" % yv)], writes=[buf("gT%d" % n)])

        for n in range(NFF):
            if n == 1 and it + 1 < nt:
                front_ln(it + 1)
            sv = next_w()
            sg = next_w()
            bv = next_bank()
            proj_chunk(bv, sv)
            bg = next_bank()
            proj_chunk(bg, sg)
            for (which, b, ch) in ((0, bv, n), (1, bg, NFF + n)):
                yi = (YV, YG)[which][n % 3]
                yb_ = buf("scr%d" % yi)
                w0 = cols[:, C_WDWF + 0 * 44 + ch:C_WDWF + 0 * 44 + ch + 1]
                w1 = cols[:, C_WDWF + 1 * 44 + ch:C_WDWF + 1 * 44 + ch + 1]
                w2 = cols[:, C_WDWF + 2 * 44 + ch:C_WDWF + 2 * 44 + ch + 1]
                bb_ = cols[:, C_BDWF + ch:C_BDWF + ch + 1]
                sc.add("act", lambda e, b=b, yi=yi, w2=w2, bb_=bb_: e.activation(
                    out=scr[:, yi, 0:T], in_=ps[:, b, :], func=AF.Identity, scale=w2, bias=bb_),
                    reads=[bank[b], buf("cols")], writes=[yb_])
                sc.add("act", lambda e, b=b, ch=ch: e.activation(
                    out=stash[:, ch, :], in_=ps[:, b, T - 2:T], func=AF.Copy),
                    reads=[bank[b]], writes=[buf("stash"), bank[b]])
                sc.add("dve", lambda e, b=b, yi=yi, w1=w1: e.scalar_tensor_tensor(
                    out=scr[:, yi, 1:T], in0=ps[:, b, 0:T - 1], scalar=w1, in1=scr[:, yi, 1:T],
                    op0=ALU.mult, op1=ALU.add),
                    reads=[bank[b], yb_, buf("cols")], writes=[yb_, bank[b]])
                sc.add("dve", lambda e, b=b, yi=yi, w0=w0: e.scalar_tensor_tensor(
                    out=scr[:, yi, 2:T], in0=ps[:, b, 0:T - 2], scalar=w0, in1=scr[:, yi, 2:T],
                    op0=ALU.mult, op1=ALU.add),
                    reads=[bank[b], yb_, buf("cols")], writes=[yb_])
                if it > 0:
                    sc.add("pool", lambda e, yi=yi, ch=ch: e.tensor_tensor(
                        out=scr[:, yi, 0:2], in0=scr[:, yi, 0:2], in1=carry[:, ch, :], op=ALU.add),
                        reads=[buf("carry"), yb_], writes=[yb_])
            if n > 1:
                ffn_finish(n - 2)
        ffn_finish(NFF - 2)
        ffn_finish(NFF - 1)
        if it + 1 < nt:
            to_feature_major(0, 1)
        if it < nt - 1:
            w0t = cols[:, C_WDWF:C_WDWF + 2 * NFF]
            w1t = cols[:, C_WDWF + 2 * NFF:C_WDWF + 4 * NFF]
            sc.add("pool", lambda e: e.tensor_tensor(out=carry[:, :, 0], in0=stash[:, :, 0], in1=w0t, op=ALU.mult),
                   reads=[buf("stash"), buf("cols")], writes=[buf("carry")])
            sc.add("pool", lambda e: e.tensor_tensor(out=ctmp[:, :], in0=stash[:, :, 1], in1=w1t, op=ALU.mult),
                   reads=[buf("stash"), buf("cols")], writes=[buf("ctmp")])
            sc.add("pool", lambda e: e.tensor_tensor(out=carry[:, :, 0], in0=carry[:, :, 0], in1=ctmp[:, :], op=ALU.add),
                   reads=[buf("carry"), buf("ctmp")], writes=[buf("carry")])
            sc.add("pool", lambda e: e.tensor_tensor(out=carry[:, :, 1], in0=stash[:, :, 1], in1=w0t, op=ALU.mult),
                   reads=[buf("stash"), buf("cols")], writes=[buf("carry")])
        for n in range(NFF):
            slot = next_w()
            for s in range(NSUB):
                for h in range(2):
                    b = acc[s * 2 + h]
                    sc.add("pe", lambda e, n=n, s=s, h=h, b=b, slot=slot: e.matmul(
                        ps[:, b, :], lhsT=gT3[:, n, s * P:(s + 1) * P], rhs=ring[:, slot, h * 512:(h + 1) * 512],
                        start=(n == 0), stop=(n == NFF - 1)),
                        reads=[ring_buf[slot], buf("gT%d" % n)], writes=[bank[b]])
        for s in range(NSUB):
            for h in range(2):
                b = acc[s * 2 + h]
                sc.add("dve", lambda e, s=s, h=h, b=b: e.tensor_tensor(
                    out=scr[:, 2 + h, 0:T], in0=ps[:, b, :], in1=g2b[:, h * 512:(h + 1) * 512], op=ALU.mult),
                    reads=[bank[b], buf("gb")], writes=[buf("scr%d" % (2 + h))])
                sc.add("dve", lambda e, s=s, h=h: e.scalar_tensor_tensor(
                    out=xt[:, s, h * 512:(h + 1) * 512], in0=xt[:, s, h * 512:(h + 1) * 512], scalar=ALPHA,
                    in1=scr[:, 2 + h, 0:T], op0=ALU.mult, op1=ALU.add),
                    reads=[xtbs[s], buf("scr%d" % (2 + h))], writes=[xtbs[s]])
        ln_stats(xt, xtbs)
        for s in range(NSUB):
            sc.add("dve", lambda e, s=s: e.scalar_tensor_tensor(
                out=xt[:, s, :], in0=xt[:, s, :], scalar=mv[:, s, 0:1], in1=lnb[:, 2, :],
                op0=ALU.subtract, op1=ALU.mult),
                reads=[xtbs[s], buf("mv"), buf("lnb")], writes=[xtbs[s]])
            sc.add("dve", lambda e, s=s: e.scalar_tensor_tensor(
                out=xt[:, s, :], in0=xt[:, s, :], scalar=rstd4[:, s:s + 1], in1=lnb[:, 3, :],
                op0=ALU.mult, op1=ALU.add),
                reads=[xtbs[s], buf("rstd4"), buf("lnb")], writes=[xtbs[s]])
        sc.add("sp", lambda e, t0=t0: e.dma_start(
            out=out_d[t0:t0 + T, :].rearrange("(s p) d -> p s d", p=P), in_=xt[:, :, :]),
            reads=xtbs, chan="o%d" % (it % 2))


    for it in range(nt):
        do_tile(it)

    sc.finalize()
    handles = {"pe": "tensor", "act": "scalar", "dve": "vector", "pool": "gpsimd", "sp": "sync"}

    def tok(d):
        if d.chan is not None:
            return chan_sem(d.chan), d.chanval
        return sems[d.eng], d.sigval

    for name in sc.chan_cnt:
        chan_sem(name)

    with nc.Block() as block:
        def emit(engname):
            def body(e):
                waited = {}
                for op in sc.ops[engname]:
                    need = []
                    for d in op.waits:
                        sem, val = tok(d)
                        key = id(sem)
                        if waited.get(key, 0) >= val:
                            continue
                        waited[key] = val
                        need = [w for w in need if w[0] is not sem] + [(sem, val)]
                    for (sem, val) in need[:-1]:
                        e.wait_ge(sem, val)
                    ins = op.fn(e)
                    if need:
                        ins._wait_ge(need[-1][0], need[-1][1])
                    if op.chan is not None:
                        ins.then_inc(chan_sem(op.chan), 16)
                    elif op.signal:
                        ins.then_inc(sems[engname], 1)
                if engname == "sp":
                    for name, cnt in sc.chan_cnt.items():
                        e.wait_ge(chan_sem(name), cnt)
            return body

        block.tensor(emit("pe"))
        block.scalar(emit("act"))
        block.vector(emit("dve"))
        block.gpsimd(emit("pool"))
        block.sync(emit("sp"))
    es.close()
    return nc


def _colmajor(v, nchunk):
    return np.ascontiguousarray(v.reshape(nchunk, P).T)


def _prep_shared(w_ada, b_ada, w_in, w_dw_a, b_dw_a, ln_a_g, ln_a_b, w_pw_a, w_pool, pool_scale, w_out,
                 ln1_g, ln1_b, w_up, w_dw_f, b_dw_f, w_down, ln2_g, ln2_b):
    f = np.float32
    cols = np.zeros((P, NCOLS), f)
    cols[:, C_BDWA:C_BDWA + 4] = _colmajor(b_dw_a[0], 4)
    cols[:, C_LNAG:C_LNAG + 4] = _colmajor(ln_a_g[0], 4)
    cols[:, C_LNAB:C_LNAB + 4] = _colmajor(ln_a_b[0], 4)
    cols[:, C_PSC:C_PSC + 8] = _colmajor(pool_scale[0], 8)
    cols[:, C_WDWA:C_WDWA + 124] = w_dw_a[0].reshape(CONV_K, 4, P).transpose(2, 0, 1).reshape(P, 124)
    cols[:, C_WDWF:C_WDWF + 132] = w_dw_f[0].reshape(3, 44, P).transpose(2, 0, 1).reshape(P, 132)
    cols[:, C_BDWF:C_BDWF + 44] = _colmajor(b_dw_f[0], 44)
    rc = np.zeros((4, 16), f)
    for g in range(4):
        w = 2 << g
        for t in range(w - 1):
            rc[g, t] = 1.0 / float(t + 1)
    cols[:, C_RCNT:C_RCNT + 64] = np.broadcast_to(rc.reshape(1, 64), (P, 64))

    def chunk_cols(wm, n):
        return wm[:, n * P:(n + 1) * P].reshape(KC, P, P).transpose(1, 0, 2).reshape(P, KC * P)

    win = w_in[0]
    wup = w_up[0]
    chunks = []
    for n in range(4):
        chunks.append(chunk_cols(win, n))
        chunks.append(chunk_cols(win, 4 + n))
    for n in range(8, 12):
        chunks.append(chunk_cols(win, n))
    for j in range(8):
        chunks.append(chunk_cols(win, 12 + j))
        chunks.append(chunk_cols(win, 20 + j))
    for kc in range(8):
        chunks.append(w_out[0][kc * P:(kc + 1) * P, :])
    for n in range(NFF):
        chunks.append(chunk_cols(wup, n))
        chunks.append(chunk_cols(wup, NFF + n))
    for n in range(NFF):
        chunks.append(w_down[0][n * P:(n + 1) * P, :])
    wstream = np.ascontiguousarray(np.concatenate(chunks, axis=0)).astype(f, copy=False)
    assert wstream.shape == (NCHUNK_TILE * P, D)
    wada = np.ascontiguousarray(
        w_ada[0].reshape(KC, P, 6, D).transpose(1, 2, 0, 3).reshape(P, 6, KC * D))
    bada = np.ascontiguousarray(b_ada[0].reshape(6, D))
    wpw = np.ascontiguousarray(w_pw_a[0].reshape(4, P, D).transpose(1, 0, 2).reshape(P, 4 * D))
    wpool = np.ascontiguousarray(w_pool[0].transpose(1, 0, 2).reshape(P, 4 * 256))
    lnv = np.ascontiguousarray(np.stack([ln1_g[0], ln1_b[0], ln2_g[0], ln2_b[0]], axis=0))
    return dict(cols=cols, wada=wada, bada=bada, wstream=wstream, wpw=wpw, wpool=wpool, lnv=lnv,
                ident=np.eye(P, dtype=f))


def kernel(x, c, w_ada, b_ada, w_in, w_dw_a, b_dw_a, ln_a_g, ln_a_b, w_pw_a, w_pool, pool_scale, w_out,
           ln1_g, ln1_b, w_up, w_dw_f, b_dw_f, w_down, ln2_g, ln2_b):
    args = [np.asarray(a, dtype=np.float32) for a in (
        w_ada, b_ada, w_in, w_dw_a, b_dw_a, ln_a_g, ln_a_b, w_pw_a, w_pool, pool_scale, w_out,
        ln1_g, ln1_b, w_up, w_dw_f, b_dw_f, w_down, ln2_g, ln2_b)]
    x = np.asarray(x, dtype=np.float32)
    c = np.asarray(c, dtype=np.float32)
    shared = _prep_shared(*args)
    nb = x.shape[0]
    in_maps = []
    for b in range(nb):
        m = dict(shared)
        cols = shared["cols"].copy()
        cols[:, C_C:C_C + KC] = _colmajor(c[b], KC)
        m["cols"] = cols
        m["x"] = np.ascontiguousarray(x[b])
        in_maps.append(m)
    nc = build_nc()
    res = run_bass_kernel_spmd(nc, in_maps, core_ids=list(range(nb)))
    out = np.stack([np.asarray(r["out"]) for r in res.results], axis=0)
    return out.astype(np.float32, copy=False)
```

```python
import numpy as np
from contextlib import ExitStack
import concourse.bass as bass
import concourse.mybir as mybir
from concourse.bass_utils import run_bass_kernel_spmd

F32 = mybir.dt.float32
BF16 = mybir.dt.bfloat16
AF = mybir.ActivationFunctionType
ALU = mybir.AluOpType

P = 128
D = 1024
S = 4096
T = 512
NT = S // T
NSUB = T // P
KC = D // P
CONV_K = 31
DFF = 2816
NFF = DFF // P
ALPHA = 2.0 ** 0.25
EPS = 1e-5
NRING = 7
NCHUNK_TILE = 12 + 16 + 8 + 44 + 22

C_BDWA = 0
C_LNAG = 4
C_LNAB = 8
C_PSC = 12
C_WDWA = 20
C_WDWF = 144
C_BDWF = 276
C_C = 320
C_RCNT = 328
NCOLS = 392


class Buf:
    __slots__ = ("name", "w", "rs")

    def __init__(self, name):
        self.name = name
        self.w = None
        self.rs = []


class Op:
    __slots__ = ("eng", "fn", "waits", "signal", "sigval", "chan", "chanval")

    def __init__(self, eng, fn):
        self.eng = eng
        self.fn = fn
        self.waits = []
        self.signal = False
        self.sigval = 0
        self.chan = None
        self.chanval = 0


ENGS = ("pe", "act", "dve", "pool", "sp")


class Sched:
    def __init__(self):
        self.ops = {e: [] for e in ENGS}
        self.chan_cnt = {}

    def add(self, eng, fn, reads=(), writes=(), chan=None):
        op = Op(eng, fn)
        if chan is not None:
            op.chan = chan
            self.chan_cnt[chan] = self.chan_cnt.get(chan, 0) + 16
            op.chanval = self.chan_cnt[chan]
        deps = []
        for b in reads:
            if b.w is not None:
                deps.append((b.w, True))
        for b in writes:
            if b.w is not None:
                deps.append((b.w, False))
            last = {}
            for r in b.rs:
                last[r.chan if r.chan is not None else r.eng] = r
            for r in last.values():
                deps.append((r, False))
        seen = set()
        for (d, raw) in deps:
            if id(d) in seen:
                continue
            if d.chan is None and op.chan is None and d.eng == eng:
                if eng == "pe":
                    continue
            seen.add(id(d))
            if d.chan is None:
                d.signal = True
            op.waits.append(d)
        for b in reads:
            b.rs.append(op)
        for b in writes:
            b.w = op
            b.rs = []
        self.ops[eng].append(op)
        return op

    def finalize(self):
        for e in ENGS:
            n = 0
            for op in self.ops[e]:
                if op.chan is None and op.signal:
                    n += 1
                    op.sigval = n


def build_nc(nt=NT):
    S = nt * T
    nc = bass.Bass("TRN2", target_bir_lowering=False)

    def din(name, shape, dt=F32):
        return nc.dram_tensor(name, list(shape), dt, kind="ExternalInput").ap()

    x_d = din("x", [S, D])
    cols_d = din("cols", [P, NCOLS])
    ident_d = din("ident", [P, P])
    wada_d = din("wada", [P, 6, KC * D])
    bada_d = din("bada", [6, D])
    wst_d = din("wstream", [NCHUNK_TILE * P, D])
    wpw_d = din("wpw", [P, 4 * D])
    wpool_d = din("wpool", [P, D])
    lnv_d = din("lnv", [4, D])
    out_d = nc.dram_tensor("out", [S, D], F32, kind="ExternalOutput").ap()
    wsc_d = nc.dram_tensor("wsc", [NCHUNK_TILE * P, D], BF16, kind="Internal").ap()

    sc = Sched()
    es = ExitStack()

    def sb(name, shape, dt=F32):
        return es.enter_context(nc.sbuf_tensor("sb_" + name, list(shape), dt))

    cols = sb("cols", [P, NCOLS])
    identf = sb("identf", [P, P])
    identb = sb("identb", [P, P], BF16)
    ones_f = sb("ones_f", [P, P])
    ones_row = sb("ones_row", [1, P])
    eps_t = sb("eps_t", [P, 1])
    mhalf_t = sb("mhalf_t", [P, 1])
    diag = sb("diag", [P, CONV_K * 4, P], BF16)
    g1b = sb("g1b", [P, D])
    g2b = sb("g2b", [P, D])
    lnb = sb("lnb", [P, 4, D])
    modcols = sb("modcols", [P, 4, KC])
    wpw = sb("wpw", [P, 4, D], BF16)
    wpool = sb("wpool", [P, D], BF16)
    s_bf = sb("s_bf", [P, KC], BF16)
    xt_all = sb("xt", [P, 2, NSUB, D])
    xn = sb("xn", [P, NSUB, D], BF16)
    hT = sb("hT", [P, KC, T + 2], BF16)
    glu = sb("glu", [P, 4, T + 30], BF16)
    ybuf = sb("ybuf", [P, 4, T])
    mrow = ybuf[0:1, 0:2, :].rearrange("p a b -> p (a b)")
    brow = ybuf[0:1, 2:4, :].rearrange("p a b -> p (a b)")
    scr = sb("scr", [P, 8, T + 16])
    zT = sb("zT", [P, 4, T], BF16)
    ubuf = sb("ubuf", [P, 4, T + 15])
    pooledT = sb("pooledT", [P, 4, T], BF16)
    merged = sb("merged", [P, KC, T], BF16)
    gT = sb("gT", [P, NFF * T], BF16)
    stash = sb("stash", [P, 2 * NFF, 2])
    carry = sb("carry", [P, 2 * NFF, 2])
    ctmp = sb("ctmp", [P, 2 * NFF])
    ring = sb("ring", [P, NRING, D], BF16)
    st = sb("st", [P, NSUB, 2, 6])
    mv = sb("mv", [P, NSUB, 2])
    sums = sb("sums", [P, NSUB, 2])
    sd4 = sb("sd4", [P, NSUB])
    rstd4 = sb("rstd4", [P, NSUB])
    nb4 = sb("nb4", [P, NSUB])
    ps = es.enter_context(nc.psum_tensor("ps", [P, 8, 512], F32))

    sems = {e: es.enter_context(nc.semaphore("sem_" + e)) for e in ENGS}
    chan_sems = {}

    def chan_sem(name):
        if name not in chan_sems:
            chan_sems[name] = es.enter_context(nc.semaphore("ch_" + name))
        return chan_sems[name]

    B = {}

    def buf(name):
        if name not in B:
            B[name] = Buf(name)
        return B[name]

    bank = [buf("bank%d" % i) for i in range(8)]
    bank_rr = [0]

    def next_bank():
        b = bank_rr[0]
        bank_rr[0] = (b + 1) % 8
        return b

    gT3 = gT[:, :].rearrange("p (n t) -> p n t", n=NFF)
    stg = gT[:, 0:KC * D].rearrange("p (k n) -> p k n", k=KC)
    stgb = [buf("gT%d" % q) for q in range(KC * D // T)]
    glub = [buf("glu%d" % q) for q in range(4)]
    ubufb = [buf("ubuf%d" % q) for q in range(4)]

    sc.add("sp", lambda e: e.dma_start(out=cols[:, :], in_=cols_d), writes=[buf("cols")], chan="su0")
    sc.add("sp", lambda e: e.dma_start(out=identf[:, :], in_=ident_d), writes=[buf("identf")], chan="su1")
    for i in range(4):
        sc.add("sp", lambda e, i=i: e.dma_start(out=lnb[:, i, :], in_=lnv_d[i:i + 1, :].broadcast_to([P, D])),
               writes=[buf("lnb")], chan="su2")
    sc.add("pool", lambda e: e.memset(ones_f[:, :], 1.0 / 512.0), writes=[buf("ones_f")])
    sc.add("pool", lambda e: e.memset(ones_row[:, :], 1.0), writes=[buf("ones_row")])
    sc.add("pool", lambda e: e.memset(eps_t[:, :], EPS), writes=[buf("eps_t")])
    sc.add("pool", lambda e: e.memset(mhalf_t[:, :], -0.5), writes=[buf("mhalf_t")])
    sc.add("pool", lambda e: e.dma_start(out=wpw[:, :, :], in_=wpw_d.rearrange("p (c d) -> p c d", c=4)),
           writes=[buf("wpw")], chan="sg0")
    sc.add("pool", lambda e: e.dma_start(out=wpool[:, :], in_=wpool_d), writes=[buf("wpool")], chan="sg1")
    PIECE_CH = 3
    NPIECE = NCHUNK_TILE // PIECE_CH
    wsc_piece = [buf("wscp%d" % i) for i in range(NPIECE)]
    cast_state = {"n": 0}

    def cast_piece(_i, count=6):
        for _ in range(count):
            i = cast_state["n"]
            if i >= NPIECE:
                return
            cast_state["n"] = i + 1
            r0 = i * PIECE_CH * P
            r1 = (i + 1) * PIECE_CH * P
            sc.add("pool", lambda e, r0=r0, r1=r1: e.dma_start(out=wsc_d[r0:r1, :], in_=wst_d[r0:r1, :]),
                   writes=[wsc_piece[i]], chan="wc%d" % i)

    sc.add("act", lambda e: e.activation(out=identb[:, :], in_=identf[:, :], func=AF.Copy),
           reads=[buf("identf")], writes=[buf("identb")])
    sc.add("act", lambda e: e.activation(out=s_bf[:, :], in_=cols[:, C_C:C_C + KC], func=AF.Silu),
           reads=[buf("cols")], writes=[buf("s_bf")])

    def mod_issue(p):
        sc.add("pool", lambda e: e.dma_start(out=stg, in_=wada_d[:, p, :].rearrange("p (k n) -> p k n", k=KC)),
               writes=stgb, chan="wada")

    def mod_consume(p):
        sc.add("sp", lambda e: e.dma_start(out=brow, in_=bada_d[p:p + 1, :]), writes=[buf("ybuf2"), buf("ybuf3")], chan="su3")
        for h in range(2):
            b = next_bank()
            for kc in range(KC):
                sc.add("pe", lambda e, b=b, kc=kc, h=h: e.matmul(
                    ps[0:1, b, :], lhsT=s_bf[:, kc:kc + 1], rhs=stg[:, kc, h * 512:(h + 1) * 512],
                    start=(kc == 0), stop=(kc == KC - 1)),
                    reads=[buf("s_bf")] + stgb, writes=[bank[b]])
            sc.add("dve", lambda e, b=b, h=h: e.tensor_tensor(
                out=mrow[:, h * 512:(h + 1) * 512], in0=ps[0:1, b, :], in1=brow[:, h * 512:(h + 1) * 512],
                op=ALU.add), reads=[bank[b], buf("ybuf2"), buf("ybuf3")], writes=[buf("ybuf0"), buf("ybuf1")])
        if p in (2, 5):
            dst = g1b if p == 2 else g2b
            for h in range(2):
                b = next_bank()
                sc.add("pe", lambda e, b=b, h=h: e.matmul(
                    ps[:, b, :], lhsT=ones_row[0:1, :], rhs=mrow[:, h * 512:(h + 1) * 512], start=True, stop=True),
                    reads=[buf("ones_row"), buf("ybuf0"), buf("ybuf1")], writes=[bank[b]])
                sc.add("act", lambda e, b=b, h=h, dst=dst: e.activation(
                    out=dst[:, h * 512:(h + 1) * 512], in_=ps[:, b, :], func=AF.Copy),
                    reads=[bank[b]], writes=[buf("gb")])
        else:
            idx = {0: 1, 1: 0, 3: 3, 4: 2}[p]
            b = next_bank()
            for kc in range(KC):
                sc.add("pe", lambda e, b=b, kc=kc: e.matmul(
                    ps[:, b, kc:kc + 1], lhsT=mrow[:, kc * P:(kc + 1) * P], rhs=ones_row[0:1, 0:1],
                    start=True, stop=True),
                    reads=[buf("ones_row"), buf("ybuf0"), buf("ybuf1")], writes=[bank[b]])
            addc = 1.0 if p in (1, 4) else 0.0
            sc.add("dve", lambda e, b=b, idx=idx, addc=addc: e.tensor_scalar(
                out=modcols[:, idx, :], in0=ps[:, b, 0:KC], scalar1=addc, scalar2=None, op0=ALU.add),
                reads=[bank[b]], writes=[buf("modcols")])

    mod_issue(0)
    cast_piece(0, 4)
    for j in range(CONV_K):
        for c in range(4):
            i = j * 4 + c
            sc.add("act", lambda e, i=i: e.activation(
                out=diag[:, i, :], in_=identf[:, :], func=AF.Copy, scale=cols[:, C_WDWA + i:C_WDWA + i + 1]),
                reads=[buf("identf"), buf("cols")], writes=[buf("diag")])

    mod_consume(0)
    mod_issue(1)
    cast_piece(0, 8)
    mod_consume(1)
    mod_issue(2)
    cast_piece(0, 22)

    ring_state = {"n": 0}
    ring_buf = [buf("ring%d" % i) for i in range(NRING)]

    def ring_load():
        n = ring_state["n"]
        ring_state["n"] = n + 1
        slot = n % NRING
        k = stream_order[n]
        piece = wsc_piece[k // PIECE_CH]
        sc.add("sp", lambda e: e.dma_start(out=ring[:, slot, :], in_=wsc_d[k * P:(k + 1) * P, :]),
               reads=[piece], writes=[ring_buf[slot]], chan="ring%d" % slot)
        return slot

    pending = []
    PREFETCH = 5
    stream_order = []
    for _i in range(nt):
        stream_order += list(range(0, 36))
        if _i > 0:
            stream_order += list(range(80, 102))
        stream_order += list(range(36, 80))
    stream_order += list(range(80, 102))
    total_chunks = len(stream_order)

    def next_w():
        while len(pending) < PREFETCH + 1 and ring_state["n"] < total_chunks:
            pending.append(ring_load())
        return pending.pop(0)

    def ln_stats(xt, xtbs):
        for s in range(NSUB):
            sc.add("act", lambda e, s=s: e.activation(out=xn[:, s, :], in_=xt[:, s, :], func=AF.Copy,
                                                       accum_out=sums[:, s, 0:1]),
                   reads=[xtbs[s]], writes=[buf("xn"), buf("sums")])
            sc.add("act", lambda e, s=s: e.activation(out=xn[:, s, :], in_=xt[:, s, :], func=AF.Square,
                                                       accum_out=sums[:, s, 1:2]),
                   reads=[xtbs[s]], writes=[buf("xn"), buf("sums")])
        sc.add("act", lambda e: e.activation(out=mv[:, :, 0], in_=sums[:, :, 0], func=AF.Copy, scale=1.0 / D),
               reads=[buf("sums")], writes=[buf("mv")])
        sc.add("dve", lambda e: e.tensor_tensor(out=sd4[:, :], in0=mv[:, :, 0], in1=mv[:, :, 0], op=ALU.mult),
               reads=[buf("mv")], writes=[buf("sd4")])
        sc.add("dve", lambda e: e.scalar_tensor_tensor(
            out=mv[:, :, 1], in0=sums[:, :, 1], scalar=1.0 / D, in1=sd4[:, :], op0=ALU.mult, op1=ALU.subtract),
            reads=[buf("sums"), buf("sd4")], writes=[buf("mv")])
        sc.add("act", lambda e: e.activation(out=sd4[:, :], in_=mv[:, :, 1], func=AF.Ln,
                                              bias=eps_t[:, 0:1], scale=1.0),
               reads=[buf("mv"), buf("eps_t")], writes=[buf("sd4")])
        sc.add("act", lambda e: e.activation(out=rstd4[:, :], in_=sd4[:, :], func=AF.Exp, scale=-0.5),
               reads=[buf("sd4")], writes=[buf("rstd4")])
        sc.add("dve", lambda e: e.scalar_tensor_tensor(
            out=nb4[:, :], in0=mv[:, :, 0], scalar=-1.0, in1=rstd4[:, :], op0=ALU.mult, op1=ALU.mult),
            reads=[buf("mv"), buf("rstd4")], writes=[buf("nb4")])

    def to_feature_major(mod_scale_idx, mod_shift_idx):
        for kc in range(KC):
            b = next_bank()
            pb = ps[:, b, :].bitcast(BF16)
            for s in range(NSUB):
                sc.add("pe", lambda e, pb=pb, s=s, kc=kc: e.transpose(
                    pb[:, s * P:(s + 1) * P], xn[:, s, kc * P:(kc + 1) * P], identb[:, :]),
                    reads=[buf("xn"), buf("identb")], writes=[bank[b]])
            eng = "act" if kc % 2 == 0 else "dve"
            if eng == "act":
                sc.add("act", lambda e, pb=pb, kc=kc: e.activation(
                    out=hT[:, kc, 2:2 + T], in_=pb[:, 0:T], func=AF.Identity,
                    scale=modcols[:, mod_scale_idx, kc:kc + 1], bias=modcols[:, mod_shift_idx, kc:kc + 1]),
                    reads=[bank[b], buf("modcols")], writes=[buf("hT%d" % kc)])
            else:
                sc.add("dve", lambda e, pb=pb, kc=kc: e.tensor_scalar(
                    out=hT[:, kc, 2:2 + T], in0=pb[:, 0:T],
                    scalar1=modcols[:, mod_scale_idx, kc:kc + 1], scalar2=modcols[:, mod_shift_idx, kc:kc + 1],
                    op0=ALU.mult, op1=ALU.add),
                    reads=[bank[b], buf("modcols")], writes=[buf("hT%d" % kc)])

    hT_bufs = [buf("hT%d" % kc) for kc in range(KC)]

    def proj_chunk(b, slot, rhs_lo=2):
        for kc in range(KC):
            sc.add("pe", lambda e, kc=kc: e.matmul(
                ps[:, b, :], lhsT=ring[:, slot, kc * P:(kc + 1) * P], rhs=hT[:, kc, rhs_lo:rhs_lo + T],
                start=(kc == 0), stop=(kc == KC - 1)),
                reads=[ring_buf[slot], hT_bufs[kc]], writes=[bank[b]])

    def front_ln(it):
        t0 = it * T
        xt = xt_all[:, it % 2]
        xtbs = [buf("xt%d_%d" % (it % 2, q)) for q in range(NSUB)]
        sc.add("sp", lambda e, t0=t0: e.dma_start(
            out=xt[:, :, :], in_=x_d[t0:t0 + T, :].rearrange("(s p) d -> p s d", p=P)),
            writes=xtbs, chan="x%d" % (it % 2))
        ln_stats(xt, xtbs)
        for s in range(NSUB):
            sc.add("act", lambda e, s=s: e.activation(
                out=xn[:, s, :], in_=xt[:, s, :], func=AF.Identity, scale=rstd4[:, s:s + 1], bias=nb4[:, s:s + 1]),
                reads=[xtbs[s], buf("rstd4"), buf("nb4")], writes=[buf("xn")])

    def ffn_down(it, stage):
        t0 = it * T
        xt = xt_all[:, it % 2]
        xtbs = [buf("xt%d_%d" % (it % 2, q)) for q in range(NSUB)]
        acc = list(range(8))
        if stage == "pe":
            for n in range(NFF):
                slot = next_w()
                for s in range(NSUB):
                    for h in range(2):
                        b = acc[s * 2 + h]
                        sc.add("pe", lambda e, n=n, s=s, h=h, b=b, slot=slot: e.matmul(
                            ps[:, b, :], lhsT=gT3[:, n, s * P:(s + 1) * P], rhs=ring[:, slot, h * 512:(h + 1) * 512],
                            start=(n == 0), stop=(n == NFF - 1)),
                            reads=[ring_buf[slot], buf("gT%d" % n)], writes=[bank[b]])
        elif stage == "evac":
            for s in range(NSUB):
                for h in range(2):
                    b = acc[s * 2 + h]
                    sc.add("dve", lambda e, s=s, h=h, b=b: e.tensor_tensor(
                        out=scr[:, 2 + h, 0:T], in0=ps[:, b, :], in1=g2b[:, h * 512:(h + 1) * 512], op=ALU.mult),
                        reads=[bank[b], buf("gb")], writes=[buf("scr%d" % (2 + h))])
                    sc.add("dve", lambda e, s=s, h=h: e.scalar_tensor_tensor(
                        out=xt[:, s, h * 512:(h + 1) * 512], in0=xt[:, s, h * 512:(h + 1) * 512], scalar=ALPHA,
                        in1=scr[:, 2 + h, 0:T], op0=ALU.mult, op1=ALU.add),
                        reads=[xtbs[s], buf("scr%d" % (2 + h))], writes=[xtbs[s]])
        else:
            ln_stats(xt, xtbs)
            for s in range(NSUB):
                sc.add("dve", lambda e, s=s: e.scalar_tensor_tensor(
                    out=xt[:, s, :], in0=xt[:, s, :], scalar=mv[:, s, 0:1], in1=lnb[:, 2, :],
                    op0=ALU.subtract, op1=ALU.mult),
                    reads=[xtbs[s], buf("mv"), buf("lnb")], writes=[xtbs[s]])
                sc.add("dve", lambda e, s=s: e.scalar_tensor_tensor(
                    out=xt[:, s, :], in0=xt[:, s, :], scalar=rstd4[:, s:s + 1], in1=lnb[:, 3, :],
                    op0=ALU.mult, op1=ALU.add),
                    reads=[xtbs[s], buf("rstd4"), buf("lnb")], writes=[xtbs[s]])
            sc.add("sp", lambda e, t0=t0: e.dma_start(
                out=out_d[t0:t0 + T, :].rearrange("(s p) d -> p s d", p=P), in_=xt[:, :, :]),
                reads=xtbs, chan="o%d" % (it % 2))

    def do_tile(it):
        t0 = it * T
        xt = xt_all[:, it % 2]
        xtbs = [buf("xt%d_%d" % (it % 2, q)) for q in range(NSUB)]
        if it == 0:
            front_ln(0)
            to_feature_major(0, 1)

        if it == 0:
            sc.add("pool", lambda e: e.memset(glu[:, :, 0:30], 0.0), writes=glub)
            sc.add("pool", lambda e: e.memset(ubuf[:, :, 0:15], 0.0), writes=ubufb)
        else:
            sc.add("pool", lambda e: e.tensor_copy(out=glu[:, :, 0:30], in_=glu[:, :, T:T + 30]),
                   reads=glub, writes=glub)
            sc.add("pool", lambda e: e.tensor_copy(out=ubuf[:, :, 0:15], in_=ubuf[:, :, T:T + 15]),
                   reads=ubufb, writes=ubufb)

        for c in range(4):
            bv = next_bank()
            proj_chunk(bv, next_w())
            bg = next_bank()
            proj_chunk(bg, next_w())
            sc.add("act", lambda e, bg=bg, c=c: e.activation(out=scr[:, c % 2, 0:T], in_=ps[:, bg, :], func=AF.Sigmoid),
                   reads=[bank[bg]], writes=[buf("scr%d" % (c % 2))])
            sc.add("dve", lambda e, bv=bv, c=c: e.tensor_tensor(
                out=glu[:, c, 30:30 + T], in0=ps[:, bv, :], in1=scr[:, c % 2, 0:T], op=ALU.mult),
                reads=[bank[bv], buf("scr%d" % (c % 2))], writes=[glub[c]])
        for g in range(4):
            slot = next_w()
            b = next_bank()
            proj_chunk(b, slot)
            sc.add("act", lambda e, b=b, g=g: e.activation(out=ubuf[:, g, 15:15 + T], in_=ps[:, b, :], func=AF.Copy),
                   reads=[bank[b]], writes=[ubufb[g]])
        if it == 0:
            mod_consume(2)
            mod_issue(3)
        for c in range(4):
            b = next_bank()
            for j in range(CONV_K):
                sc.add("pe", lambda e, b=b, c=c, j=j: e.matmul(
                    ps[:, b, :], lhsT=diag[:, j * 4 + c, :], rhs=glu[:, c, j:j + T],
                    start=(j == 0), stop=(j == CONV_K - 1)),
                    reads=[buf("diag"), glub[c]], writes=[bank[b]])
            sc.add("act", lambda e, b=b, c=c: e.activation(
                out=ybuf[:, c, :], in_=ps[:, b, :], func=AF.Identity, bias=cols[:, C_BDWA + c:C_BDWA + c + 1], scale=1.0),
                reads=[bank[b], buf("cols")], writes=[buf("ybuf%d" % c)])
            sc.add("act", lambda e, b=b, c=c: e.activation(
                out=scr[:, c % 2, 0:T], in_=ps[:, b, :], func=AF.Square, bias=cols[:, C_BDWA + c:C_BDWA + c + 1], scale=1.0),
                reads=[bank[b], buf("cols")], writes=[buf("scr%d" % (c % 2))])
            if c == 0:
                bm = next_bank()
                bq = next_bank()
            sc.add("pe", lambda e, c=c, bm=bm: e.matmul(
                ps[:, bm, :], lhsT=ones_f[:, :], rhs=ybuf[:, c, :], start=(c == 0), stop=(c == 3)),
                reads=[buf("ones_f"), buf("ybuf%d" % c)], writes=[bank[bm]])
            sc.add("pe", lambda e, c=c, bq=bq: e.matmul(
                ps[:, bq, :], lhsT=ones_f[:, :], rhs=scr[:, c % 2, 0:T], start=(c == 0), stop=(c == 3)),
                reads=[buf("ones_f"), buf("scr%d" % (c % 2))], writes=[bank[bq]])
        sc.add("act", lambda e, bm=bm: e.activation(out=scr[:, 4, 0:T], in_=ps[:, bm, :], func=AF.Copy),
               reads=[bank[bm]], writes=[buf("scr4")])
        sc.add("act", lambda e, bm=bm: e.activation(out=scr[:, 2, 0:T], in_=ps[:, bm, :], func=AF.Square),
               reads=[bank[bm]], writes=[buf("scr2")])
        sc.add("dve", lambda e, bq=bq: e.tensor_tensor(out=scr[:, 3, 0:T], in0=ps[:, bq, :], in1=scr[:, 2, 0:T],
                                                        op=ALU.subtract),
               reads=[bank[bq], buf("scr2")], writes=[buf("scr3")])
        sc.add("act", lambda e: e.activation(out=scr[:, 2, 0:T], in_=scr[:, 3, 0:T], func=AF.Ln,
                                              bias=eps_t[:, 0:1], scale=1.0),
               reads=[buf("scr3"), buf("eps_t")], writes=[buf("scr2")])
        sc.add("act", lambda e: e.activation(out=scr[:, 5, 0:T], in_=scr[:, 2, 0:T], func=AF.Exp, scale=-0.5),
               reads=[buf("scr2")], writes=[buf("scr5")])
        for c in range(4):
            sc.add("dve", lambda e, c=c: e.tensor_tensor(out=scr[:, 2 + c % 2, 0:T], in0=ybuf[:, c, :], in1=scr[:, 4, 0:T],
                                                          op=ALU.subtract),
                   reads=[buf("ybuf%d" % c), buf("scr4")], writes=[buf("scr%d" % (2 + c % 2))])
            sc.add("dve", lambda e, c=c: e.tensor_tensor(out=scr[:, 2 + c % 2, 0:T], in0=scr[:, 2 + c % 2, 0:T], in1=scr[:, 5, 0:T],
                                                          op=ALU.mult),
                   reads=[buf("scr%d" % (2 + c % 2)), buf("scr5")], writes=[buf("scr%d" % (2 + c % 2))])
            sc.add("act", lambda e, c=c: e.activation(
                out=zT[:, c, :], in_=scr[:, 2 + c % 2, 0:T], func=AF.Silu,
                scale=cols[:, C_LNAG + c:C_LNAG + c + 1], bias=cols[:, C_LNAB + c:C_LNAB + c + 1]),
                reads=[buf("scr%d" % (2 + c % 2)), buf("cols")], writes=[buf("zT%d" % c)])
        for g in range(4):
            w = 2 << g
            src = None
            for k in range(g + 1):
                sh = 1 << k
                lo = (2 << k) - 1
                dst = scr[:, 6 + k % 2, 0:T + 15]
                a = ubuf[:, g, :] if k == 0 else scr[:, 6 + (k - 1) % 2, 0:T + 15]
                rd = [ubufb[g]] if k == 0 else [buf("scr%d" % (6 + (k - 1) % 2))]
                sc.add("pool", lambda e, dst=dst, a=a, lo=lo, sh=sh: e.tensor_tensor(
                    out=dst[:, lo:T + 15], in0=a[:, lo:T + 15], in1=a[:, lo - sh:T + 15 - sh], op=ALU.add),
                    reads=rd, writes=[buf("scr%d" % (6 + k % 2))])
                src = (dst, buf("scr%d" % (6 + k % 2)))
            sc.add("dve", lambda e, g=g, w=w, src=src: e.scalar_tensor_tensor(
                out=pooledT[:, g, :], in0=src[0][:, 15:15 + T], scalar=1.0 / w, in1=ubuf[:, g, 15:15 + T],
                op0=ALU.mult, op1=ALU.subtract),
                reads=[src[1], ubufb[g]], writes=[buf("pooledT%d" % g)])
            if it == 0:
                n = w - 1
                sc.add("pool", lambda e, g=g, n=n, src=src: e.tensor_tensor(
                    out=src[0][:, 15:15 + n], in0=src[0][:, 15:15 + n],
                    in1=cols[:, C_RCNT + g * 16:C_RCNT + g * 16 + n], op=ALU.mult),
                    reads=[src[1], buf("cols")], writes=[src[1]])
                sc.add("pool", lambda e, g=g, n=n, src=src: e.tensor_tensor(
                    out=pooledT[:, g, 0:n], in0=src[0][:, 15:15 + n], in1=ubuf[:, g, 15:15 + n], op=ALU.subtract),
                    reads=[src[1], ubufb[g]], writes=[buf("pooledT%d" % g)])
        for jj in range(0, KC, 2):
            gate_banks = {}
            for j in (jj, jj + 1):
                sa_slot = next_w()
                ba = next_bank()
                proj_chunk(ba, sa_slot)
                sb_slot = next_w()
                bb = next_bank()
                proj_chunk(bb, sb_slot)
                gate_banks[j] = (ba, bb)
            for j in (jj, jj + 1):
                ba, bb = gate_banks[j]
                q = 4 * (j % 2)
                bc = next_bank()
                for c in range(4):
                    sc.add("pe", lambda e, c=c, j=j, bc=bc: e.matmul(
                        ps[:, bc, :], lhsT=wpw[:, c, j * P:(j + 1) * P], rhs=zT[:, c, :], start=(c == 0), stop=(c == 3)),
                        reads=[buf("wpw"), buf("zT%d" % c)], writes=[bank[bc]])
                bd = next_bank()
                g = j // 2
                sc.add("pe", lambda e, j=j, g=g, bd=bd: e.matmul(
                    ps[:, bd, :], lhsT=wpool[:, g * 256 + (j % 2) * P:g * 256 + (j % 2) * P + P], rhs=pooledT[:, g, :],
                    start=True, stop=True),
                    reads=[buf("wpool"), buf("pooledT%d" % g)], writes=[bank[bd]])
                sc.add("act", lambda e, ba=ba, q=q: e.activation(out=scr[:, q + 0, 0:T], in_=ps[:, ba, :], func=AF.Sigmoid),
                       reads=[bank[ba]], writes=[buf("scr%d" % (q + 0))])
                sc.add("act", lambda e, bb=bb, q=q: e.activation(out=scr[:, q + 1, 0:T], in_=ps[:, bb, :], func=AF.Sigmoid),
                       reads=[bank[bb]], writes=[buf("scr%d" % (q + 1))])
                sc.add("dve", lambda e, bc=bc, q=q: e.tensor_tensor(out=scr[:, q + 2, 0:T], in0=ps[:, bc, :],
                                                                      in1=scr[:, q + 0, 0:T], op=ALU.mult),
                       reads=[bank[bc], buf("scr%d" % (q + 0))], writes=[buf("scr%d" % (q + 2))])
                sc.add("dve", lambda e, bd=bd, j=j, q=q: e.scalar_tensor_tensor(
                    out=scr[:, q + 3, 0:T], in0=ps[:, bd, :], scalar=cols[:, C_PSC + j:C_PSC + j + 1],
                    in1=scr[:, q + 1, 0:T], op0=ALU.mult, op1=ALU.mult),
                    reads=[bank[bd], buf("scr%d" % (q + 1)), buf("cols")], writes=[buf("scr%d" % (q + 3))])
                sc.add("pool", lambda e, j=j, q=q: e.tensor_tensor(out=merged[:, j, :], in0=scr[:, q + 2, 0:T],
                                                                    in1=scr[:, q + 3, 0:T], op=ALU.add),
                       reads=[buf("scr%d" % (q + 2)), buf("scr%d" % (q + 3))], writes=[buf("merged%d" % j)])
        if it == 0:
            mod_consume(3)
            mod_issue(4)
        acc = list(range(8))
        for kc in range(KC):
            slot = next_w()
            for s in range(NSUB):
                for h in range(2):
                    b = acc[s * 2 + h]
                    sc.add("pe", lambda e, kc=kc, s=s, h=h, b=b, slot=slot: e.matmul(
                        ps[:, b, :], lhsT=merged[:, kc, s * P:(s + 1) * P], rhs=ring[:, slot, h * 512:(h + 1) * 512],
                        start=(kc == 0), stop=(kc == KC - 1)),
                        reads=[ring_buf[slot], buf("merged%d" % kc)], writes=[bank[b]])
        for s in range(NSUB):
            for h in range(2):
                b = acc[s * 2 + h]
                sc.add("dve", lambda e, s=s, h=h, b=b: e.tensor_tensor(
                    out=scr[:, 2 + h, 0:T], in0=ps[:, b, :], in1=g1b[:, h * 512:(h + 1) * 512], op=ALU.mult),
                    reads=[bank[b], buf("gb")], writes=[buf("scr%d" % (2 + h))])
                sc.add("dve", lambda e, s=s, h=h: e.scalar_tensor_tensor(
                    out=xt[:, s, h * 512:(h + 1) * 512], in0=xt[:, s, h * 512:(h + 1) * 512], scalar=ALPHA,
                    in1=scr[:, 2 + h, 0:T], op0=ALU.mult, op1=ALU.add),
                    reads=[xtbs[s], buf("scr%d" % (2 + h))], writes=[xtbs[s]])
        if it == 0:
            mod_consume(4)
            mod_issue(5)
        if it > 0:
            ffn_down(it - 1, "pe")
        ln_stats(xt, xtbs)
        for s in range(NSUB):
            sc.add("dve", lambda e, s=s: e.scalar_tensor_tensor(
                out=xt[:, s, :], in0=xt[:, s, :], scalar=mv[:, s, 0:1], in1=lnb[:, 0, :],
                op0=ALU.subtract, op1=ALU.mult),
                reads=[xtbs[s], buf("mv"), buf("lnb")], writes=[xtbs[s]])
            sc.add("dve", lambda e, s=s: e.scalar_tensor_tensor(
                out=xt[:, s, :], in0=xt[:, s, :], scalar=rstd4[:, s:s + 1], in1=lnb[:, 1, :],
                op0=ALU.mult, op1=ALU.add),
                reads=[xtbs[s], buf("rstd4"), buf("lnb")], writes=[xtbs[s]])
        if it == 0:
            sc.add("pool", lambda e: e.memset(hT[:, :, 0:2], 0.0), writes=hT_bufs)
        ln_stats(xt, xtbs)
        for s in range(NSUB):
            sc.add("act", lambda e, s=s: e.activation(
                out=xn[:, s, :], in_=xt[:, s, :], func=AF.Identity, scale=rstd4[:, s:s + 1], bias=nb4[:, s:s + 1]),
                reads=[xtbs[s], buf("rstd4"), buf("nb4")], writes=[buf("xn")])
        if it > 0:
            ffn_down(it - 1, "evac")
        to_feature_major(2, 3)
        if it > 0:
            ffn_down(it - 1, "ln2")

        if it == 0:
            mod_consume(5)
        YV = (0, 1, 7)
        YG = (2, 3, 6)

        def ffn_finish(n):
            yv = YV[n % 3]
            yg = YG[n % 3]
            sc.add("act", lambda e, n=n, yg=yg: e.activation(out=scr[:, 4 + n % 2, 0:T], in_=scr[:, yg, 0:T],
                                                               func=AF.Gelu_apprx_tanh),
                   reads=[buf("scr%d" % yg)], writes=[buf("scr%d" % (4 + n % 2))])
            sc.add("pool", lambda e, n=n, yv=yv: e.tensor_tensor(
                out=gT3[:, n, :], in0=scr[:, 4 + n % 2, 0:T], in1=scr[:, yv, 0:T], op=ALU.mult),
                reads=[buf("scr%d" % (4 + n % 2)), buf("scr%d" % yv)], writes=[buf("gT%d" % n)])

        for n in range(NFF):
            if n == 6 and it + 1 < nt:
                front_ln(it + 1)
            sv = next_w()
            sg = next_w()
            bv = next_bank()
            proj_chunk(bv, sv)
            bg = next_bank()
            proj_chunk(bg, sg)
            for (which, b, ch) in ((0, bv, n), (1, bg, NFF + n)):
                yi = (YV, YG)[which][n % 3]
                yb_ = buf("scr%d" % yi)
                w0 = cols[:, C_WDWF + 0 * 44 + ch:C_WDWF + 0 * 44 + ch + 1]
                w1 = cols[:, C_WDWF + 1 * 44 + ch:C_WDWF + 1 * 44 + ch + 1]
                w2 = cols[:, C_WDWF + 2 * 44 + ch:C_WDWF + 2 * 44 + ch + 1]
                bb_ = cols[:, C_BDWF + ch:C_BDWF + ch + 1]
                sc.add("act", lambda e, b=b, yi=yi, w2=w2, bb_=bb_: e.activation(
                    out=scr[:, yi, 0:T], in_=ps[:, b, :], func=AF.Identity, scale=w2, bias=bb_),
                    reads=[bank[b], buf("cols")], writes=[yb_])
                sc.add("act", lambda e, b=b, ch=ch: e.activation(
                    out=stash[:, ch, :], in_=ps[:, b, T - 2:T], func=AF.Copy),
                    reads=[bank[b]], writes=[buf("stash"), bank[b]])
                sc.add("dve", lambda e, b=b, yi=yi, w1=w1: e.scalar_tensor_tensor(
                    out=scr[:, yi, 1:T], in0=ps[:, b, 0:T - 1], scalar=w1, in1=scr[:, yi, 1:T],
                    op0=ALU.mult, op1=ALU.add),
                    reads=[bank[b], yb_, buf("cols")], writes=[yb_, bank[b]])
                sc.add("dve", lambda e, b=b, yi=yi, w0=w0: e.scalar_tensor_tensor(
                    out=scr[:, yi, 2:T], in0=ps[:, b, 0:T - 2], scalar=w0, in1=scr[:, yi, 2:T],
                    op0=ALU.mult, op1=ALU.add),
                    reads=[bank[b], yb_, buf("cols")], writes=[yb_])
                if it > 0:
                    sc.add("pool", lambda e, yi=yi, ch=ch: e.tensor_tensor(
                        out=scr[:, yi, 0:2], in0=scr[:, yi, 0:2], in1=carry[:, ch, :], op=ALU.add),
                        reads=[buf("carry"), yb_], writes=[yb_])
            if n > 1:
                ffn_finish(n - 2)
        ffn_finish(NFF - 2)
        ffn_finish(NFF - 1)
        if it + 1 < nt:
            to_feature_major(0, 1)
        if it < nt - 1:
            w0t = cols[:, C_WDWF:C_WDWF + 2 * NFF]
            w1t = cols[:, C_WDWF + 2 * NFF:C_WDWF + 4 * NFF]
            sc.add("pool", lambda e: e.tensor_tensor(out=carry[:, :, 0], in0=stash[:, :, 0], in1=w0t, op=ALU.mult),
                   reads=[buf("stash"), buf("cols")], writes=[buf("carry")])
            sc.add("pool", lambda e: e.tensor_tensor(out=ctmp[:, :], in0=stash[:, :, 1], in1=w1t, op=ALU.mult),
                   reads=[buf("stash"), buf("cols")], writes=[buf("ctmp")])
            sc.add("pool", lambda e: e.tensor_tensor(out=carry[:, :, 0], in0=carry[:, :, 0], in1=ctmp[:, :], op=ALU.add),
                   reads=[buf("carry"), buf("ctmp")], writes=[buf("carry")])
            sc.add("pool", lambda e: e.tensor_tensor(out=carry[:, :, 1], in0=stash[:, :, 1], in1=w0t, op=ALU.mult),
                   reads=[buf("stash"), buf("cols")], writes=[buf("carry")])

    for it in range(nt):
        do_tile(it)
    ffn_down(nt - 1, "pe")
    ffn_down(nt - 1, "evac")
    ffn_down(nt - 1, "ln2")

    sc.finalize()
    handles = {"pe": "tensor", "act": "scalar", "dve": "vector", "pool": "gpsimd", "sp": "sync"}

    def tok(d):
        if d.chan is not None:
            return chan_sem(d.chan), d.chanval
        return sems[d.eng], d.sigval

    for name in sc.chan_cnt:
        chan_sem(name)

    with nc.Block() as block:
        def emit(engname):
            def body(e):
                waited = {}
                for op in sc.ops[engname]:
                    need = []
                    for d in op.waits:
                        sem, val = tok(d)
                        key = id(sem)
                        if waited.get(key, 0) >= val:
                            continue
                        waited[key] = val
                        need = [w for w in need if w[0] is not sem] + [(sem, val)]
                    for (sem, val) in need[:-1]:
                        e.wait_ge(sem, val)
                    ins = op.fn(e)
                    if need:
                        ins._wait_ge(need[-1][0], need[-1][1])
                    if op.chan is not None:
                        ins.then_inc(chan_sem(op.chan), 16)
                    elif op.signal:
                        ins.then_inc(sems[engname], 1)
                if engname == "sp":
                    for name, cnt in sc.chan_cnt.items():
                        e.wait_ge(chan_sem(name), cnt)
            return body

        block.tensor(emit("pe"))
        block.scalar(emit("act"))
        block.vector(emit("dve"))
        block.gpsimd(emit("pool"))
        block.sync(emit("sp"))
    es.close()
    return nc


def _colmajor(v, nchunk):
    return np.ascontiguousarray(v.reshape(nchunk, P).T)


def _prep_shared(w_ada, b_ada, w_in, w_dw_a, b_dw_a, ln_a_g, ln_a_b, w_pw_a, w_pool, pool_scale, w_out,
                 ln1_g, ln1_b, w_up, w_dw_f, b_dw_f, w_down, ln2_g, ln2_b):
    f = np.float32
    cols = np.zeros((P, NCOLS), f)
    cols[:, C_BDWA:C_BDWA + 4] = _colmajor(b_dw_a[0], 4)
    cols[:, C_LNAG:C_LNAG + 4] = _colmajor(ln_a_g[0], 4)
    cols[:, C_LNAB:C_LNAB + 4] = _colmajor(ln_a_b[0], 4)
    cols[:, C_PSC:C_PSC + 8] = _colmajor(pool_scale[0], 8)
    cols[:, C_WDWA:C_WDWA + 124] = w_dw_a[0].reshape(CONV_K, 4, P).transpose(2, 0, 1).reshape(P, 124)
    cols[:, C_WDWF:C_WDWF + 132] = w_dw_f[0].reshape(3, 44, P).transpose(2, 0, 1).reshape(P, 132)
    cols[:, C_BDWF:C_BDWF + 44] = _colmajor(b_dw_f[0], 44)
    rc = np.zeros((4, 16), f)
    for g in range(4):
        w = 2 << g
        for t in range(w - 1):
            rc[g, t] = 1.0 / float(t + 1)
    cols[:, C_RCNT:C_RCNT + 64] = np.broadcast_to(rc.reshape(1, 64), (P, 64))

    def chunk_cols(wm, n):
        return wm[:, n * P:(n + 1) * P].reshape(KC, P, P).transpose(1, 0, 2).reshape(P, KC * P)

    win = w_in[0]
    wup = w_up[0]
    chunks = []
    for n in range(4):
        chunks.append(chunk_cols(win, n))
        chunks.append(chunk_cols(win, 4 + n))
    for n in range(8, 12):
        chunks.append(chunk_cols(win, n))
    for j in range(8):
        chunks.append(chunk_cols(win, 12 + j))
        chunks.append(chunk_cols(win, 20 + j))
    for kc in range(8):
        chunks.append(w_out[0][kc * P:(kc + 1) * P, :])
    for n in range(NFF):
        chunks.append(chunk_cols(wup, n))
        chunks.append(chunk_cols(wup, NFF + n))
    for n in range(NFF):
        chunks.append(w_down[0][n * P:(n + 1) * P, :])
    wstream = np.ascontiguousarray(np.concatenate(chunks, axis=0)).astype(f, copy=False)
    assert wstream.shape == (NCHUNK_TILE * P, D)
    wada = np.ascontiguousarray(
        w_ada[0].reshape(KC, P, 6, D).transpose(1, 2, 0, 3).reshape(P, 6, KC * D))
    bada = np.ascontiguousarray(b_ada[0].reshape(6, D))
    wpw = np.ascontiguousarray(w_pw_a[0].reshape(4, P, D).transpose(1, 0, 2).reshape(P, 4 * D))
    wpool = np.ascontiguousarray(w_pool[0].transpose(1, 0, 2).reshape(P, 4 * 256))
    lnv = np.ascontiguousarray(np.stack([ln1_g[0], ln1_b[0], ln2_g[0], ln2_b[0]], axis=0))
    return dict(cols=cols, wada=wada, bada=bada, wstream=wstream, wpw=wpw, wpool=wpool, lnv=lnv,
                ident=np.eye(P, dtype=f))


def kernel(x, c, w_ada, b_ada, w_in, w_dw_a, b_dw_a, ln_a_g, ln_a_b, w_pw_a, w_pool, pool_scale, w_out,
           ln1_g, ln1_b, w_up, w_dw_f, b_dw_f, w_down, ln2_g, ln2_b):
    args = [np.asarray(a, dtype=np.float32) for a in (
        w_ada, b_ada, w_in, w_dw_a, b_dw_a, ln_a_g, ln_a_b, w_pw_a, w_pool, pool_scale, w_out,
        ln1_g, ln1_b, w_up, w_dw_f, b_dw_f, w_down, ln2_g, ln2_b)]
    x = np.asarray(x, dtype=np.float32)
    c = np.asarray(c, dtype=np.float32)
    shared = _prep_shared(*args)
    nb = x.shape[0]
    in_maps = []
    for b in range(nb):
        m = dict(shared)
        cols = shared["cols"].copy()
        cols[:, C_C:C_C + KC] = _colmajor(c[b], KC)
        m["cols"] = cols
        m["x"] = np.ascontiguousarray(x[b])
        in_maps.append(m)
    nc = build_nc()
    res = run_bass_kernel_spmd(nc, in_maps, core_ids=list(range(nb)))
    out = np.stack([np.asarray(r["out"]) for r in res.results], axis=0)
    return out.astype(np.float32, copy=False)
```
